# Optimizing a Trainium2 kernel written in Bass

```python
import jax, jax.numpy as jnp
from jax import lax
import numpy as np

D_MODEL = 2048
BATCH = 2
SEQ = 8192
DEPTH = 1

D_MIX = D_MODEL
HEAD_DIM = 128
D_A = D_MIX // 2
D_B = D_MIX - D_A
N_HEADS_A = D_A // HEAD_DIM
N_GROUPS_B = D_B // HEAD_DIM
CHUNK = 128
CONV_W = 3
D_FF = 5632
D_IN = 2 * D_A + 3 * D_B
LN_EPS = 1e-5
ALPHA = float((2 * DEPTH) ** 0.25)
BETA = float((8 * DEPTH) ** -0.25)

kernel_name = "hybrid_sgu_shortconv_macaron_deepnorm"


def layer_norm(x, g, b):
    xf = x.astype(jnp.float32)
    mu = jnp.mean(xf, axis=-1, keepdims=True)
    d = xf - mu
    var = jnp.mean(d * d, axis=-1, keepdims=True)
    y = d * lax.rsqrt(var + LN_EPS)
    return (y * g.astype(jnp.float32) + b.astype(jnp.float32)).astype(x.dtype)


def swiglu(x, w_gate, w_up, w_down):
    return (jax.nn.silu(x @ w_gate) * (x @ w_up)) @ w_down


def chunked_sgu(z, w_s, b_s, g_v, b_v):
    bsz, seq, _ = z.shape
    n_chunks = seq // CHUNK
    u, v = z[..., :D_A], z[..., D_A:]
    shp = (bsz, n_chunks, CHUNK, N_HEADS_A, HEAD_DIM)
    v = v.reshape(shp)
    v = layer_norm(v, g_v.reshape(N_HEADS_A, HEAD_DIM), b_v.reshape(N_HEADS_A, HEAD_DIM))
    causal = jnp.tril(jnp.ones((CHUNK, CHUNK), dtype=bool))
    w = jnp.where(causal[None], w_s, jnp.zeros((), w_s.dtype))
    mixed = jnp.einsum('hts,bnshd->bnthd', w, v)
    mixed = mixed + jnp.transpose(b_s)[:, :, None]
    return (u.reshape(shp) * mixed).reshape(bsz, seq, D_A)


def gated_short_conv(gate_b, gate_c, xt, conv_w):
    h = gate_c * xt
    rhs = conv_w[:, None, :]
    y = lax.conv_general_dilated(
        h, rhs, window_strides=(1,), padding=[(CONV_W - 1, 0)],
        dimension_numbers=('NWC', 'WIO', 'NWC'), feature_group_count=D_B)
    return gate_b * y


def setup_inputs(seed: int = 0) -> dict:
    key = jax.random.key(seed)
    ks = jax.random.split(key, 24)
    f32 = jnp.float32
    n = lambda k, shape, s: jax.random.normal(k, shape, f32) * s
    L = DEPTH
    return {
        "x": jax.random.normal(ks[0], (BATCH, SEQ, D_MODEL), f32),
        "ffa_gate": n(ks[1], (L, D_MODEL, D_FF), D_MODEL ** -0.5),
        "ffa_up": n(ks[2], (L, D_MODEL, D_FF), D_MODEL ** -0.5),
        "ffa_down": n(ks[3], (L, D_FF, D_MODEL), BETA * D_FF ** -0.5),
        "ln_a_g": 1.0 + n(ks[4], (L, D_MODEL), 0.01),
        "ln_a_b": n(ks[5], (L, D_MODEL), 0.01),
        "w_in": n(ks[6], (L, D_MODEL, D_IN), D_MODEL ** -0.5),
        "b_in": n(ks[7], (L, D_IN), 0.01),
        "w_s": n(ks[8], (L, N_HEADS_A, CHUNK, CHUNK), 0.5 * CHUNK ** -0.5),
        "b_s": 1.0 + n(ks[9], (L, N_HEADS_A, CHUNK), 0.01),
        "ln_v_g": 1.0 + n(ks[10], (L, D_A), 0.01),
        "ln_v_b": n(ks[11], (L, D_A), 0.01),
        "conv_w": n(ks[12], (L, CONV_W, D_B), CONV_W ** -0.5),
        "w_out": n(ks[13], (L, D_MIX, D_MODEL), BETA * D_MIX ** -0.5),
        "b_out": n(ks[14], (L, D_MODEL), 0.01),
        "ln_m_g": 1.0 + n(ks[15], (L, D_MODEL), 0.01),
        "ln_m_b": n(ks[16], (L, D_MODEL), 0.01),
        "ffc_gate": n(ks[17], (L, D_MODEL, D_FF), D_MODEL ** -0.5),
        "ffc_up": n(ks[18], (L, D_MODEL, D_FF), D_MODEL ** -0.5),
        "ffc_down": n(ks[19], (L, D_FF, D_MODEL), BETA * D_FF ** -0.5),
        "ln_c_g": 1.0 + n(ks[20], (L, D_MODEL), 0.01),
        "ln_c_b": n(ks[21], (L, D_MODEL), 0.01),
    }


def reference(x, ffa_gate, ffa_up, ffa_down, ln_a_g, ln_a_b, w_in, b_in, w_s, b_s,
              ln_v_g, ln_v_b, conv_w, w_out, b_out, ln_m_g, ln_m_b,
              ffc_gate, ffc_up, ffc_down, ln_c_g, ln_c_b):
    h = x
    for l in range(DEPTH):
        h = layer_norm(ALPHA * h + 0.5 * swiglu(h, ffa_gate[l], ffa_up[l], ffa_down[l]),
                       ln_a_g[l], ln_a_b[l])
        z = h @ w_in[l] + b_in[l]
        z_a = jax.nn.gelu(z[..., :2 * D_A])
        o = 2 * D_A
        gate_b = z[..., o:o + D_B]
        gate_c = z[..., o + D_B:o + 2 * D_B]
        xt = z[..., o + 2 * D_B:]
        y_a = chunked_sgu(z_a, w_s[l], b_s[l], ln_v_g[l], ln_v_b[l])
        y_b = gated_short_conv(gate_b, gate_c, xt, conv_w[l])
        mix = jnp.concatenate([y_a, y_b], axis=-1) @ w_out[l] + b_out[l]
        h = layer_norm(ALPHA * h + mix, ln_m_g[l], ln_m_b[l])
        h = layer_norm(ALPHA * h + 0.5 * swiglu(h, ffc_gate[l], ffc_up[l], ffc_down[l]),
                       ln_c_g[l], ln_c_b[l])
    return h
```

```python
import numpy as np
import concourse.bass as bass
import concourse.mybir as mybir
from concourse.bass_utils import run_bass_kernel_spmd

F32 = mybir.dt.float32
BF16 = mybir.dt.bfloat16
AF = mybir.ActivationFunctionType
ALU = mybir.AluOpType
AX = mybir.AxisListType

ALPHA = float(2.0 ** 0.25)
LN_EPS = 1e-5
GELU_C = 0.7978845608028654
FILL_LN = 36
FILL_V = 0
SCR_MOD = 3


class Cfg:
    def __init__(self, D=2048, F=5632, SEQ=8192, BATCH=2, NCORES=8, T=512):
        self.D, self.F, self.SEQ, self.BATCH, self.NCORES, self.T = D, F, SEQ, BATCH, NCORES, T
        self.DA = D // 2
        self.DB = D - self.DA
        self.NH = self.DA // 128
        self.NG = self.DB // 128
        self.DIN = 2 * self.DA + 3 * self.DB
        self.KD = D // 128
        self.KF = F // 128
        self.NTOK = BATCH * SEQ // NCORES
        self.NT = self.NTOK // T
        self.TC = T // 128
        self.KG = 4
        self.KPG = self.KF // 4
        self.CG = self.KD // 4
        self.FG = F // 512
        self.PA = self.DA // 512
        assert F % 512 == 0 and self.KF % 4 == 0 and self.KD % 4 == 0 and self.DA % 512 == 0
        assert self.KPG * 512 <= 8192 and self.KD * 512 <= 8192
        assert self.NTOK % T == 0 and SEQ % self.NTOK == 0 or self.NTOK % SEQ == 0
        off = {}
        c = 0
        for nm in ("lnag", "lnab", "lnmg", "lnmb", "lncg", "lncb", "bout"):
            off[nm] = c
            c += self.KD
        for nm, n in (("bu", self.NH), ("bB", self.NG), ("bC", self.NG), ("bX", self.NG), ("cw", 3 * self.NG), ("hm", 1)):
            off[nm] = c
            c += n
        self.off = off
        self.NCC = c


class Sched:
    def __init__(self, nc):
        self.nc = nc
        self.eng = {"pe": nc.tensor, "act": nc.scalar, "dve": nc.vector, "pool": nc.gpsimd, "sp": nc.sync}
        self.prog = {e: nc.alloc_semaphore(f"prog_{e}") for e in ("pe", "act", "dve", "pool")}
        self.cnt = {e: 0 for e in self.prog}
        self.waited = {e: {} for e in self.eng}
        self.lastw = {}
        self.readers = {}
        self.ivw = []
        self.ivr = {}
        self.dma_sems = {}
        self.nwaits = 0
        self.nops = 0

    @staticmethod
    def _is_iv(k):
        return isinstance(k, tuple) and len(k) == 4 and k[0] == "IV"

    def _deps(self, reads, writes):
        evs = []
        for k in reads:
            if self._is_iv(k):
                for ent in self.ivw:
                    if ent[0] == k[1] and ent[1] < k[3] and k[2] < ent[2]:
                        evs.append(ent[3])
            elif k in self.lastw:
                evs.append(self.lastw[k])
        for k in writes:
            if self._is_iv(k):
                for ent in self.ivw:
                    if ent[0] == k[1] and ent[1] < k[3] and k[2] < ent[2]:
                        evs.append(ent[3])
                for (nm, lo, hi, src), ev in self.ivr.items():
                    if nm == k[1] and lo < k[3] and k[2] < hi:
                        evs.append(ev)
            else:
                if k in self.lastw:
                    evs.append(self.lastw[k])
                evs.extend(self.readers.get(k, {}).values())
        return evs

    def _wait(self, e, evs):
        need = {}
        for (sem, val, src) in evs:
            if e == "pe" and src == "pe":
                continue
            key = id(sem)
            if self.waited[e].get(key, 0) >= val:
                continue
            if key not in need or need[key][1] < val:
                need[key] = (sem, val)
        for key, (sem, val) in need.items():
            self.eng[e].wait_ge(sem, val)
            self.waited[e][key] = val
            self.nwaits += 1

    def _record(self, ev, reads, writes):
        for k in writes:
            if self._is_iv(k):
                self.ivw = [ent for ent in self.ivw if not (ent[0] == k[1] and k[2] <= ent[1] and ent[2] <= k[3])]
                for rk in [rk for rk in self.ivr if rk[0] == k[1] and k[2] <= rk[1] and rk[2] <= k[3]]:
                    del self.ivr[rk]
                self.ivw.append([k[1], k[2], k[3], ev])
            else:
                self.lastw[k] = ev
                self.readers[k] = {}
        for k in reads:
            if k in writes:
                continue
            if self._is_iv(k):
                self.ivr[(k[1], k[2], k[3], ev[2] if ev[2] != "dma" else id(ev[0]))] = ev
            else:
                self.readers.setdefault(k, {})[ev[2] if ev[2] != "dma" else id(ev[0])] = ev

    def op(self, e, fn, reads=(), writes=(), milestone=True):
        self._wait(e, self._deps(reads, writes))
        ins = fn(self.eng[e])
        self.nops += 1
        if milestone:
            ins.then_inc(self.prog[e], 1)
            self.cnt[e] += 1
            ev = (self.prog[e], self.cnt[e], e)
        else:
            ev = (self.prog[e], self.cnt[e] + 1, e)
        self._record(ev, reads, writes)
        return ins

    def dma(self, e, out, in_, semname, reads=(), writes=()):
        self._wait(e, self._deps(reads, writes))
        if semname not in self.dma_sems:
            self.dma_sems[semname] = [self.nc.alloc_semaphore(f"dma_{semname}"), 0]
        ent = self.dma_sems[semname]
        ent[1] += 16
        self.eng[e].dma_start(out=out, in_=in_).then_inc(ent[0], 16)
        ev = (ent[0], ent[1], "dma")
        self._record(ev, reads, writes)
        return ev

    def wait_all(self, e, keys):
        evs = []
        for k in keys:
            if k in self.lastw:
                evs.append(self.lastw[k])
            evs.extend(self.readers.get(k, {}).values())
        self._wait(e, evs)


def IV(name, lo, hi):
    return ("IV", name, lo, hi)


class Prog:
    def __init__(self, cfg, use_halo=True):
        self.cfg = cfg
        self.use_halo = use_halo
        c = cfg
        nc = self.nc = bass.Bass("TRN2", target_bir_lowering=False)
        T, KD, KF, NH, NG = c.T, c.KD, c.KF, c.NH, c.NG
        dt = lambda n, s: nc.dram_tensor(n, s, F32, kind="ExternalInput")
        self.xT = dt("xT", [c.D, c.NTOK])
        self.xh = dt("xh", [c.D, 2])
        self.wg = {"a": dt("wg_a", [c.D, c.F]), "c": dt("wg_c", [c.D, c.F])}
        self.wu = {"a": dt("wu_a", [c.D, c.F]), "c": dt("wu_c", [c.D, c.F])}
        self.wd = {"a": dt("wd_a", [c.F, c.D]), "c": dt("wd_c", [c.F, c.D])}
        self.w_in = dt("w_in", [c.D, c.DIN])
        self.w_out = dt("w_out", [c.D, c.D])
        self.cpp_d = dt("cpp", [128, c.NCC])
        self.rows_d = dt("rows", [1, 2 * c.DA])
        self.gvbv_d = dt("gvbv", [2, c.DA])
        self.wsT_d = dt("wsT", [128, NH * 128])
        self.outT = nc.dram_tensor("outT", [c.D, c.NTOK], F32, kind="ExternalOutput")

        sb = nc.alloc_sbuf_tensor
        self.res = sb("res", [128, KD, T], F32)
        self.act = sb("act", [128, KD, T], BF16)
        self.OFF_Y = NH * T * 2
        self.OFF_X = self.OFF_Y + KD * T
        xsz = 8 * T + 2 * 2 * (T + 2)
        self.HSZ = max(KF * T, self.OFF_X + xsz)
        self.hreg = sb("hreg", [128, self.HSZ], BF16)
        self.NB = 4
        self.pan = [sb(f"pan{i}", [128, 8192], BF16) for i in range(self.NB)]
        self.tf = [sb(f"tf{i}", [128, T], F32) for i in range(6)]
        self.NRS = 3
        self.rbsq = sb("rbsq", [128, self.NRS, 2, T], BF16)
        self.cpp = sb("cppsb", [128, c.NCC], F32)
        self.cpa = sb("cpasb", [128, 4 * KD], F32)
        self.rstg = sb("rstg", [33, c.DA], F32)
        self.brow = sb("brow", [128, 2 * c.DA], BF16)
        self.ones_pad = sb("ones_pad", [128, 128], BF16)
        self.gvt = sb("gvt", [128, c.DA], F32)
        self.bvt = sb("bvt", [128, c.DA], F32)
        self.wT = sb("wTbf", [128, NH * 128], BF16)
        self.ones_bf = sb("ones_bf", [128, 128], BF16)
        self.zeros_bf = sb("zeros_bf", [128, 512], BF16)
        self.carry = sb("carry", [128, NG, 2], F32)
        self.st = sb("vstat", [128, 8, NH], F32)
        self.vtb = sb("vtb", [128, c.DA], F32)
        self.tqb = sb("tqb", [128, c.DA], F32)
        self.nbb = sb("nbb", [128, 2, c.DA], BF16)
        self.res_h = sb("res_h", [128, KD, 2], F32)
        self.act_h = sb("act_h", [128, KD, 2], BF16)
        self.hid_h = sb("hid_h", [128, KF, 2], BF16)
        self.sgh = sb("sgh", [128, 2, 2], F32)
        self.th = sb("th", [128, KD, 2], F32)
        self.rbsq_h = sb("rbsq_h", [128, 2, KD, 2], BF16)
        self.lnh = sb("lnh", [128, 4, 2], F32)
        self.ch = sb("ch", [128, 4, 2], F32)
        self.thx = sb("thx", [128, 2, 2], F32)
        self.ps = [nc.alloc_psum_tensor(f"ps{i}", [128, 512], F32) for i in range(8)]
        self.S = Sched(nc)
        self.pan_i = 0
        self.rs_i = 0
        self.NPAN = 2 * (2 * c.FG + c.CG * c.KG) + 5 * c.PA + c.CG
        self.use_scr = c.NT >= 2
        self.scr = nc.dram_tensor("wscr", [self.NPAN, 128, 8192], BF16) if self.use_scr else None
        self.scr_valid = set()

    def hid(self, m):
        T = self.cfg.T
        return self.hreg[:, m * T:(m + 1) * T], IV("h", m * T, (m + 1) * T)

    def uT(self, h, lo=0, hi=None):
        T = self.cfg.T
        hi = T if hi is None else hi
        v = self.hreg[:, 0:self.OFF_Y].bitcast(F32)
        return v[:, h * T + lo:h * T + hi], IV("h", 2 * (h * T + lo), 2 * (h * T + hi))

    def yT(self, cc, lo=0, hi=None):
        T = self.cfg.T
        hi = T if hi is None else hi
        o = self.OFF_Y + cc * T
        return self.hreg[:, o + lo:o + hi], IV("h", o + lo, o + hi)

    def xreg_f32(self, off_f32, n):
        o = self.OFF_X + 2 * off_f32
        return self.hreg[:, o:o + 2 * n].bitcast(F32), IV("h", o, o + 2 * n)

    def xreg_bf(self, off_bf, n):
        o = self.OFF_X + off_bf
        return self.hreg[:, o:o + n], IV("h", o, o + n)

    def cT(self, j):
        return self.xreg_f32(j * self.cfg.T, self.cfg.T)

    def hc(self, s):
        T = self.cfg.T
        return self.xreg_f32(4 * T + s * (T + 2), T + 2)

    def vt(self):
        return self.vtb[:], "vt"

    def tq(self):
        return self.tqb[:], "tq"

    def nbf(self, s):
        return self.nbb[:, s, :], ("nb", s)

    def col(self, name, i=0):
        o = self.cfg.off[name] + i
        return self.cpp[:, o:o + 1]

    def load_panel(self, dram_ap, kk, ncols, noscr=False):
        s = self.pan_i % self.NB
        self.pan_i += 1
        if noscr:
            pid = -1
        else:
            pid = self.pid
            self.pid += 1
        n = kk * ncols
        flat = self.pan[s][:, 0:n]
        view = flat.rearrange("p (k n) -> p k n", k=kk)
        if self.scr is not None and pid in self.scr_valid:
            self.S.dma("pool", flat, self.scr.ap()[pid][:, 0:n], f"pan{s}", reads=[("scr", pid)], writes=[("pan", s)])
        else:
            src = dram_ap.rearrange("(k p) n -> p k n", p=128)
            self.S.dma("pool", view, src, f"pan{s}", writes=[("pan", s)])
            if self.scr is not None and pid >= 0 and self.conv_mod is not None and pid % self.conv_mod[0] == self.conv_mod[1]:
                self.S.dma("sp", self.scr.ap()[pid][:, 0:n], flat, f"st{s}", reads=[("pan", s)], writes=[("scr", pid)])
                self.scr_new.append(pid)
        return view, ("pan", s)

    def setup(self):
        c, S, nc = self.cfg, self.S, self.nc
        S.dma("sp", self.cpp[:], self.cpp_d.ap(), "c_cpp", writes=["cpp"])
        S.op("pool", lambda e: e.memset(self.ones_bf[:], 1.0 / c.D), writes=["ones_bf"])
        S.op("pool", lambda e: e.memset(self.zeros_bf[:], 0.0), writes=["zeros_bf"])
        S.op("pool", lambda e: e.memset(self.carry[:], 0.0), writes=["carry"])

    def setup_late(self):
        c, S, nc = self.cfg, self.S, self.nc
        NH = c.NH
        S.op("pool", lambda e: e.memset(self.brow[:], 0.0), writes=["rows"])
        S.op("pool", lambda e: e.memset(self.ones_pad[:], 0.0), writes=["ones_pad"])
        S.op("pool", lambda e: e.memset(self.ones_pad[0:1, :], 1.0), reads=["ones_pad"], writes=["ones_pad"])
        S.op("pool", lambda e: e.memset(self.ones_pad[32:33, :], 1.0), reads=["ones_pad"], writes=["ones_pad"])
        for hh in range(2):
            seg = slice(hh * c.DA, (hh + 1) * c.DA)
            S.dma("sp", self.rstg[0:1, :], self.rows_d.ap()[0:1, seg], "c_rows0", writes=["rstg0"])
            S.dma("sp", self.rstg[32:33, :], self.rows_d.ap()[0:1, seg], "c_rows1", writes=["rstg1"])
            S.op("dve", lambda e: e.tensor_copy(out=self.brow[0:1, seg], in_=self.rstg[0:1, :]), reads=["rstg0", "rows"], writes=["rows"])
            S.op("dve", lambda e: e.tensor_copy(out=self.brow[32:33, seg], in_=self.rstg[32:33, :]), reads=["rstg1", "rows"], writes=["rows"])
            S.op("dve", lambda e: e.tensor_tensor(out=self.rstg[32:33, :], in0=self.rstg[32:33, :], in1=self.brow[32:33, seg],
                                                  op=ALU.subtract), reads=["rstg1", "rows"], writes=["rstg1"])
            S.op("dve", lambda e: e.tensor_copy(out=self.brow[32:33, seg], in_=self.rstg[32:33, :]), reads=["rstg1", "rows"], writes=["rows"])
        S.dma("sp", self.gvt[:], self.gvbv_d.ap()[0:1, :].partition_broadcast(128), "c_gv", writes=["gvt"])
        S.dma("sp", self.bvt[:], self.gvbv_d.ap()[1:2, :].partition_broadcast(128), "c_bv", writes=["bvt"])
        wst = self.vtb[:]
        S.dma("sp", wst, self.wsT_d.ap(), "c_ws", writes=["vt"])
        for h in range(NH):
            S.op("pool", lambda e: e.affine_select(out=wst[:, h * 128:(h + 1) * 128], in_=wst[:, h * 128:(h + 1) * 128],
                                                   pattern=[[1, 128]], compare_op=ALU.is_ge, fill=0.0, base=0,
                                                   channel_multiplier=-1), reads=["vt"], writes=["vt"])
        S.op("pool", lambda e: e.tensor_copy(out=self.wT[:], in_=wst), reads=["vt"], writes=["wT"])
        o = c.off
        S.op("dve", lambda e: e.tensor_scalar(out=self.cpa[:, 0:4 * c.KD], in0=self.cpp[:, o["lnag"]:o["lnag"] + 4 * c.KD],
                                              scalar1=ALPHA, scalar2=None, op0=ALU.mult), reads=["cpp"], writes=["cpa"])

    def gateup_evac(self, m, halo_ps=None):
        S, T = self.S, self.cfg.T
        bg, bu = m % 2, 2 + m % 2
        sg = self.tf[m % 2]
        S.op("act", lambda e: e.activation(out=sg[:], in_=self.ps[bg][:, 0:T], func=AF.Silu),
             reads=[("ps", bg)], writes=[("tf", m % 2)])
        hv, hk = self.hid(m)
        S.op("dve", lambda e: e.tensor_tensor(out=hv, in0=sg[:], in1=self.ps[bu][:, 0:T], op=ALU.mult),
             reads=[("tf", m % 2), ("ps", bu)], writes=[hk])

    def halo_load(self):
        S = self.S
        hsrc = self.xh.ap().rearrange("(k p) n -> p k n", p=128)
        S.dma("pool", self.act_h[:], hsrc, "xhact", writes=["act_h"])
        S.dma("sp", self.res_h[:], hsrc, "xhres", writes=["res_h"])

    def halo_gateup(self, m, pg, kg_, pu, ku_, cs):
        S, KD = self.S, self.cfg.KD
        bgh, buh = 4 + m % 2, 6 + m % 2
        for k in range(KD):
            S.op("pe", lambda e: e.matmul(self.ps[bgh][:, 0:2], pg[:, k, cs], self.act_h[:, k, :],
                                          start=(k == 0), stop=(k == KD - 1)),
                 reads=[kg_, "act_h"], writes=[("ps", bgh)], milestone=(k == KD - 1))
        for k in range(KD):
            S.op("pe", lambda e: e.matmul(self.ps[buh][:, 0:2], pu[:, k, cs], self.act_h[:, k, :],
                                          start=(k == 0), stop=(k == KD - 1)),
                 reads=[ku_, "act_h"], writes=[("ps", buh)], milestone=(k == KD - 1))
        S.op("act", lambda e: e.activation(out=self.sgh[:, m % 2, :], in_=self.ps[bgh][:, 0:2], func=AF.Silu),
             reads=[("ps", bgh)], writes=[("sgh", m % 2)])
        S.op("dve", lambda e: e.tensor_tensor(out=self.hid_h[:, m, :], in0=self.sgh[:, m % 2, :],
                                              in1=self.ps[buh][:, 0:2], op=ALU.mult),
             reads=[("sgh", m % 2), ("ps", buh)], writes=[("hid_h", m)])

    def after_mchunk(self, m):
        if self.deferred:
            self.deferred.pop(0)()
        elif self.res_state == 0:
            self.res_load()
            self.res_state = 1
            self.res_m = m

    def ffn(self, which, scale_res, halo=False, kouter_first=False, fill=0, first_tile=False):
        c, S = self.cfg, self.S
        T, KD = c.T, c.KD
        wg, wu, wd = self.wg[which].ap(), self.wu[which].ap(), self.wd[which].ap()
        for j in range(c.FG):
            if j == 0 and first_tile:
                for mm in range(4):
                    pgs, kgs = self.load_panel(wg[:, mm * 128:(mm + 1) * 128], KD, 128, noscr=True)
                    pus, kus = self.load_panel(wu[:, mm * 128:(mm + 1) * 128], KD, 128, noscr=True)
                    if mm == 0:
                        self.pid += 2
                    m = mm
                    bg, bu = m % 2, 2 + m % 2
                    for k in range(KD):
                        S.op("pe", lambda e: e.matmul(self.ps[bg][:, 0:T], pgs[:, k, :], self.act[:, k, :],
                                                      start=(k == 0), stop=(k == KD - 1)),
                             reads=[kgs, ("act", k)], writes=[("ps", bg)], milestone=(k == KD - 1))
                    for k in range(KD):
                        S.op("pe", lambda e: e.matmul(self.ps[bu][:, 0:T], pus[:, k, :], self.act[:, k, :],
                                                      start=(k == 0), stop=(k == KD - 1)),
                             reads=[kus, ("act", k)], writes=[("ps", bu)], milestone=(k == KD - 1))
                    self.gateup_evac(m)
                    if halo:
                        if mm == 0:
                            self.halo_load()
                        self.halo_gateup(m, pgs, kgs, pus, kus, slice(0, 128))
                    self.after_mchunk(m)
                if not self.late_done:
                    self.setup_late()
                    self.late_done = True
                if c.FG > 1:
                    continue
                pg = pu = None
                mm_list = []
            else:
                pg, kg_ = self.load_panel(wg[:, j * 512:(j + 1) * 512], KD, 512)
                pu, ku_ = self.load_panel(wu[:, j * 512:(j + 1) * 512], KD, 512)
                mm_list = list(range(4))
            if kouter_first and j == 0:
                for k in range(KD):
                    for mm in (0, 1):
                        for (pan, pk_, bank) in ((pg, kg_, mm % 2), (pu, ku_, 2 + mm % 2)):
                            S.op("pe", lambda e: e.matmul(self.ps[bank][:, 0:T], pan[:, k, mm * 128:(mm + 1) * 128], self.act[:, k, :],
                                                          start=(k == 0), stop=(k == KD - 1)),
                                 reads=[pk_, ("act", k)], writes=[("ps", bank)], milestone=(k == KD - 1))
                for mm in (0, 1):
                    self.gateup_evac(mm)
                mm_list = [2, 3]
            for mm in mm_list:
                m = 4 * j + mm
                bg, bu = m % 2, 2 + m % 2
                for k in range(KD):
                    S.op("pe", lambda e: e.matmul(self.ps[bg][:, 0:T], pg[:, k, mm * 128:(mm + 1) * 128], self.act[:, k, :],
                                                  start=(k == 0), stop=(k == KD - 1)),
                         reads=[kg_, ("act", k)], writes=[("ps", bg)], milestone=(k == KD - 1))
                for k in range(KD):
                    S.op("pe", lambda e: e.matmul(self.ps[bu][:, 0:T], pu[:, k, mm * 128:(mm + 1) * 128], self.act[:, k, :],
                                                  start=(k == 0), stop=(k == KD - 1)),
                         reads=[ku_, ("act", k)], writes=[("ps", bu)], milestone=(k == KD - 1))
                self.gateup_evac(m)
                if halo:
                    self.halo_gateup(m, pg, kg_, pu, ku_, slice(mm * 128, (mm + 1) * 128))
                self.after_mchunk(m)
            if j == 0 and not self.late_done:
                self.setup_late()
                self.late_done = True
            if j == c.FG - 1:
                while self.deferred:
                    self.deferred.pop(0)()
                if self.res_state == 0:
                    self.res_load()
                    self.res_state = 1
                    self.res_m = -100
            if scale_res and self.res_state == 1 and j == c.FG - 1:
                self.res_state = 2
                for cc in range(KD):
                    S.op("dve", lambda e: e.tensor_scalar(out=self.res[:, cc, :], in0=self.res[:, cc, :], scalar1=ALPHA,
                                                          scalar2=None, op0=ALU.mult),
                         reads=[("res", cc)], writes=[("res", cc)])
                if halo:
                    S.op("dve", lambda e: e.tensor_scalar(out=self.res_h[:], in0=self.res_h[:], scalar1=ALPHA, scalar2=None,
                                                          op0=ALU.mult), reads=["res_h"], writes=["res_h"])
        assert not self.deferred and (not scale_res or self.res_state == 2)
        self.stat_first = True
        self.hblk = 0
        pending = []

        def evac(cc, bank):
            S.op("dve", lambda e: e.scalar_tensor_tensor(out=self.res[:, cc, :], in0=self.ps[bank][:, 0:T], scalar=0.5,
                                                         in1=self.res[:, cc, :], op0=ALU.mult, op1=ALU.add),
                 reads=[("ps", bank), ("res", cc)], writes=[("res", cc)])
            self.stat_prep(cc)
            pending.append(cc)

        for cg in range(c.CG):
            for kg in range(c.KG):
                pd, kd_ = self.load_panel(wd[kg * c.KPG * 128:(kg + 1) * c.KPG * 128, cg * 512:(cg + 1) * 512], c.KPG, 512)
                for c4 in range(4):
                    cc = 4 * cg + c4
                    bank = 4 + c4
                    for kk in range(c.KPG):
                        k = kg * c.KPG + kk
                        hv, hk = self.hid(k)
                        S.op("pe", lambda e: e.matmul(self.ps[bank][:, 0:T], pd[:, kk, c4 * 128:(c4 + 1) * 128], hv,
                                                      start=(kg == 0 and kk == 0), stop=(kg == c.KG - 1 and kk == c.KPG - 1)),
                             reads=[kd_, hk], writes=[("ps", bank)], milestone=(kk == c.KPG - 1))
                    if halo:
                        bh = 2 + self.hblk % 2
                        self.hblk += 1
                        for kk in range(c.KPG):
                            k = kg * c.KPG + kk
                            S.op("pe", lambda e: e.matmul(self.ps[bh][:, 0:2], pd[:, kk, c4 * 128:(c4 + 1) * 128], self.hid_h[:, k, :],
                                                          start=(kk == 0), stop=(kk == c.KPG - 1)),
                                 reads=[kd_, ("hid_h", k)], writes=[("ps", bh)], milestone=(kk == c.KPG - 1))
                        S.op("dve", lambda e: e.scalar_tensor_tensor(out=self.res_h[:, cc, :], in0=self.ps[bh][:, 0:2], scalar=0.5,
                                                                     in1=self.res_h[:, cc, :], op0=ALU.mult, op1=ALU.add),
                             reads=[("ps", bh), "res_h"], writes=["res_h"])
                    while pending:
                        self.stat_mm(pending.pop(0), 0, 1)
                    if kg == c.KG - 1:
                        evac(cc, bank)
        while pending:
            self.stat_mm(pending.pop(0), 0, 1)
        if halo:
            self.halo_ln()
        if fill:
            self.fillers(fill, 3)

    def halo_ln(self):
        c, S = self.cfg, self.S
        KD = c.KD
        S.op("act", lambda e: e.activation(out=self.rbsq_h[:, 0, :, :], in_=self.res_h[:], func=AF.Copy),
             reads=["res_h"], writes=["rbh"])
        S.op("act", lambda e: e.activation(out=self.rbsq_h[:, 1, :, :], in_=self.res_h[:], func=AF.Square),
             reads=["res_h"], writes=["sqh"])
        for cc in range(KD):
            S.op("pe", lambda e: e.matmul(self.ps[2][:, 0:2], self.ones_bf[:], self.rbsq_h[:, 0, cc, :], start=(cc == 0),
                                          stop=(cc == KD - 1)), reads=["rbh", "ones_bf"], writes=[("ps", 2)], milestone=(cc == KD - 1))
        for cc in range(KD):
            S.op("pe", lambda e: e.matmul(self.ps[3][:, 0:2], self.ones_bf[:], self.rbsq_h[:, 1, cc, :], start=(cc == 0),
                                          stop=(cc == KD - 1)), reads=["sqh", "ones_bf"], writes=[("ps", 3)], milestone=(cc == KD - 1))
        mean, var, rstd = self.lnh[:, 0, :], self.lnh[:, 1, :], self.lnh[:, 2, :]
        S.op("dve", lambda e: e.tensor_copy(out=mean, in_=self.ps[2][:, 0:2]), reads=[("ps", 2)], writes=["lnh"])
        S.op("dve", lambda e: e.tensor_tensor(out=var, in0=mean, in1=mean, op=ALU.mult), reads=["lnh"], writes=["lnh"])
        S.op("dve", lambda e: e.scalar_tensor_tensor(out=var, in0=self.ps[3][:, 0:2], scalar=LN_EPS, in1=var, op0=ALU.add,
                                                     op1=ALU.subtract), reads=[("ps", 3), "lnh"], writes=["lnh"])
        S.op("act", lambda e: e.activation(out=var, in_=var, func=AF.Sqrt), reads=["lnh"], writes=["lnh"])
        S.op("dve", lambda e: e.reciprocal(out=rstd, in_=var), reads=["lnh"], writes=["lnh"])
        mb = mean.unsqueeze(1).to_broadcast([128, KD, 2])
        rb = rstd.unsqueeze(1).to_broadcast([128, KD, 2])
        go, bo = c.off["lnag"], c.off["lnab"]
        gb = self.cpp[:, go:go + KD].unsqueeze(2).to_broadcast([128, KD, 2])
        bb = self.cpp[:, bo:bo + KD].unsqueeze(2).to_broadcast([128, KD, 2])
        S.op("dve", lambda e: e.tensor_tensor(out=self.th[:], in0=self.res_h[:], in1=mb, op=ALU.subtract),
             reads=["res_h", "lnh"], writes=["th"])
        S.op("dve", lambda e: e.tensor_tensor(out=self.th[:], in0=self.th[:], in1=rb, op=ALU.mult), reads=["th", "lnh"], writes=["th"])
        S.op("dve", lambda e: e.tensor_tensor(out=self.th[:], in0=self.th[:], in1=gb, op=ALU.mult), reads=["th", "cpp"], writes=["th"])
        S.op("dve", lambda e: e.tensor_tensor(out=self.act_h[:], in0=self.th[:], in1=bb, op=ALU.add), reads=["th", "cpp"],
             writes=["act_h"])

    def stat_prep(self, cc):
        S, T = self.S, self.cfg.T
        s = self.rs_i % self.NRS
        self.rs_i += 1
        S.op("act", lambda e: e.activation(out=self.rbsq[:, s, 0, :], in_=self.res[:, cc, :], func=AF.Copy),
             reads=[("res", cc)], writes=[("rb", s)])
        S.op("act", lambda e: e.activation(out=self.rbsq[:, s, 1, :], in_=self.res[:, cc, :], func=AF.Square),
             reads=[("res", cc)], writes=[("sq", s)])
        self.stat_slot[cc] = s

    def stat_mm(self, cc, b1, b2):
        S, T, KD = self.S, self.cfg.T, self.cfg.KD
        s = self.stat_slot[cc]
        first = self.stat_first
        self.stat_n = 1 if first else self.stat_n + 1
        self.stat_first = False
        last = (self.stat_n == KD)
        S.op("pe", lambda e: e.matmul(self.ps[b1][:, 0:T], self.ones_bf[:], self.rbsq[:, s, 0, :], start=first, stop=last),
             reads=[("rb", s), "ones_bf"], writes=[("ps", b1)], milestone=True)
        S.op("pe", lambda e: e.matmul(self.ps[b2][:, 0:T], self.ones_bf[:], self.rbsq[:, s, 1, :], start=first, stop=last),
             reads=[("sq", s), "ones_bf"], writes=[("ps", b2)], milestone=True)

    def fillers(self, n, bank):
        S = self.S
        for i in range(n):
            S.op("pe", lambda e: e.matmul(self.ps[bank][:, 0:512], self.zeros_bf[:, 0:128], self.zeros_bf[:, :], start=True, stop=True),
                 reads=["zeros_bf"], writes=[("ps", bank)], milestone=(i == n - 1))

    def layernorm(self, b1, b2, gname, bname, final, t0):
        c, S = self.cfg, self.S
        T, KD = c.T, c.KD
        mean, var, rstd = self.tf[2], self.tf[3], self.tf[4]
        S.op("dve", lambda e: e.tensor_copy(out=mean[:], in_=self.ps[b1][:, 0:T]), reads=[("ps", b1)], writes=[("tf", 2)])
        S.op("dve", lambda e: e.tensor_tensor(out=var[:], in0=mean[:], in1=mean[:], op=ALU.mult),
             reads=[("tf", 2)], writes=[("tf", 3)])
        S.op("dve", lambda e: e.scalar_tensor_tensor(out=var[:], in0=self.ps[b2][:, 0:T], scalar=LN_EPS, in1=var[:],
                                                     op0=ALU.add, op1=ALU.subtract),
             reads=[("ps", b2), ("tf", 3)], writes=[("tf", 3)])
        S.op("act", lambda e: e.activation(out=var[:], in_=var[:], func=AF.Sqrt), reads=[("tf", 3)], writes=[("tf", 3)])
        S.op("dve", lambda e: e.reciprocal(out=rstd[:], in_=var[:]), reads=[("tf", 3)], writes=[("tf", 4)])
        go, bo = c.off[gname], c.off[bname]
        ao = {"lnag": 0, "lnab": KD, "lnmg": 2 * KD, "lnmb": 3 * KD}
        tslots = (5, 3) if final else (0, 1)

        def apply(cc):
            ts = tslots[cc % 2]
            t = self.tf[ts]
            S.op("dve", lambda e: e.tensor_tensor(out=t[:], in0=self.res[:, cc, :], in1=mean[:], op=ALU.subtract),
                 reads=[("res", cc), ("tf", 2)], writes=[("tf", ts)])
            S.op("dve", lambda e: e.tensor_tensor(out=t[:], in0=t[:], in1=rstd[:], op=ALU.mult),
                 reads=[("tf", ts), ("tf", 4)], writes=[("tf", ts)])
            if final:
                S.op("act", lambda e: e.activation(out=self.res[:, cc, :], in_=t[:], func=AF.Identity,
                                                   bias=self.cpp[:, bo + cc:bo + cc + 1], scale=self.cpp[:, go + cc:go + cc + 1]),
                     reads=[("tf", ts), "cpp"], writes=[("res", cc)])
            else:
                ag, ab = ao[gname] + cc, ao[bname] + cc
                S.op("act", lambda e: e.activation(out=self.act[:, cc, :], in_=t[:], func=AF.Identity,
                                                   bias=self.cpp[:, bo + cc:bo + cc + 1], scale=self.cpp[:, go + cc:go + cc + 1]),
                     reads=[("tf", ts), "cpp"], writes=[("act", cc)])
                S.op("act", lambda e: e.activation(out=self.res[:, cc, :], in_=t[:], func=AF.Identity,
                                                   bias=self.cpa[:, ab:ab + 1], scale=self.cpa[:, ag:ag + 1]),
                     reads=[("tf", ts), "cpa"], writes=[("res", cc)])

        def store():
            dst = self.outT.ap()[:, t0:t0 + T].rearrange("(k p) n -> p k n", p=128)
            S.dma("sp", dst, self.res[:], "out", reads=[("res", cc) for cc in range(KD)])

        if final:
            for cc in range(KD):
                self.deferred.append(lambda cc=cc: apply(cc))
            self.deferred.append(store)
        else:
            for cc in range(KD):
                apply(cc)

    def zchunk_mm(self, pv, pk, jj, bank):
        S, T, KD = self.S, self.cfg.T, self.cfg.KD
        for k in range(KD):
            S.op("pe", lambda e: e.matmul(self.ps[bank][:, 0:T], pv[:, k, jj * 128:(jj + 1) * 128], self.act[:, k, :],
                                          start=(k == 0), stop=(k == KD - 1)),
                 reads=[pk, ("act", k)], writes=[("ps", bank)], milestone=(k == KD - 1))

    def mixer(self, tile_idx, halo=False):
        c, S = self.cfg, self.S
        T, KD, NH, NG, DA, PA, TC = c.T, c.KD, c.NH, c.NG, c.DA, c.PA, c.TC
        win = self.w_in.ap()
        self.zb = 0

        def nextbank():
            b = self.zb % 4
            self.zb += 1
            return b

        def u_evac(h, bank):
            xs, qs = h % 2, 2 + h % 2
            tx, tq_ = self.tf[xs], self.tf[qs]
            S.op("act", lambda e: e.activation(out=tx[:], in_=self.ps[bank][:, 0:T], func=AF.Identity,
                                               bias=self.col("bu", h), scale=1.0),
                 reads=[("ps", bank), "cpp"], writes=[("tf", xs)])
            S.op("act", lambda e: e.activation(out=tq_[:], in_=tx[:], func=AF.Square), reads=[("tf", xs)], writes=[("tf", qs)])
            S.op("dve", lambda e: e.tensor_scalar(out=tq_[:], in0=tq_[:], scalar1=0.044715, scalar2=1.0, op0=ALU.mult,
                                                  op1=ALU.add), reads=[("tf", qs)], writes=[("tf", qs)])
            S.op("dve", lambda e: e.tensor_tensor(out=tq_[:], in0=tq_[:], in1=tx[:], op=ALU.mult),
                 reads=[("tf", qs), ("tf", xs)], writes=[("tf", qs)])
            S.op("act", lambda e: e.activation(out=tq_[:], in_=tq_[:], func=AF.Sigmoid, scale=2.0 * GELU_C),
                 reads=[("tf", qs)], writes=[("tf", qs)])
            uv, uk = self.uT(h)
            S.op("dve", lambda e: e.tensor_tensor(out=uv, in0=tq_[:], in1=tx[:], op=ALU.mult),
                 reads=[("tf", qs), ("tf", xs)], writes=[uk])

        for p in range(PA):
            pv, pk = self.load_panel(win[:, p * 512:(p + 1) * 512], KD, 512)
            if p == 0:
                banks = [nextbank() for _ in range(4)]
                for k in range(KD):
                    for jj in range(4):
                        S.op("pe", lambda e: e.matmul(self.ps[banks[jj]][:, 0:T], pv[:, k, jj * 128:(jj + 1) * 128], self.act[:, k, :],
                                                      start=(k == 0), stop=(k == KD - 1)),
                             reads=[pk, ("act", k)], writes=[("ps", banks[jj])], milestone=(k == KD - 1))
                for jj in range(4):
                    u_evac(4 * p + jj, banks[jj])
            else:
                for jj in range(4):
                    bank = nextbank()
                    self.zchunk_mm(pv, pk, jj, bank)
                    u_evac(4 * p + jj, bank)

        vp = [self.load_panel(win[:, DA + p * 512:DA + (p + 1) * 512], KD, 512) for p in range(PA)]
        vt, vtk = self.vt()
        tqv, tqk = self.tq()
        st = self.st

        def vbanks(tc):
            return [4 + 2 * (tc % 2) + p for p in range(PA)]

        def V(tc):
            for p in range(PA):
                pv, pk = vp[p]
                bank = vbanks(tc)[p]
                for k in range(KD):
                    S.op("pe", lambda e: e.matmul(self.ps[bank][:, 0:512], self.act[:, k, tc * 128:(tc + 1) * 128], pv[:, k, :],
                                                  start=(k == 0), stop=False),
                         reads=[pk, ("act", k)], writes=[("ps", bank)], milestone=False)
                S.op("pe", lambda e: e.matmul(self.ps[bank][:, 0:512], self.ones_pad[:, :],
                                              self.brow[:, p * 512:(p + 1) * 512], start=False, stop=True),
                     reads=["ones_pad", "rows"], writes=[("ps", bank)], milestone=True)

        def chain(tc):
            banks = vbanks(tc)
            nb, nbk = self.nbf(tc % 2)
            for p in range(PA):
                bank = banks[p]
                sl = slice(p * 512, (p + 1) * 512)
                S.op("act", lambda e: e.activation(out=tqv[:, sl], in_=self.ps[bank][:, 0:512], func=AF.Square),
                     reads=[("ps", bank)], writes=[tqk])
                S.op("dve", lambda e: e.tensor_scalar(out=tqv[:, sl], in0=tqv[:, sl], scalar1=0.044715, scalar2=1.0,
                                                      op0=ALU.mult, op1=ALU.add), reads=[tqk], writes=[tqk])
                S.op("dve", lambda e: e.tensor_tensor(out=tqv[:, sl], in0=tqv[:, sl], in1=self.ps[bank][:, 0:512], op=ALU.mult),
                     reads=[tqk, ("ps", bank)], writes=[tqk])
                S.op("act", lambda e: e.activation(out=tqv[:, sl], in_=tqv[:, sl], func=AF.Sigmoid, scale=2.0 * GELU_C),
                     reads=[tqk], writes=[tqk])
                S.op("dve", lambda e: e.tensor_tensor(out=vt[:, sl], in0=tqv[:, sl], in1=self.ps[bank][:, 0:512], op=ALU.mult),
                     reads=[tqk, ("ps", bank)], writes=[vtk])
            v3 = vt.rearrange("p (h d) -> p h d", h=NH)
            q3 = tqv.rearrange("p (h d) -> p h d", h=NH)
            S.op("dve", lambda e: e.tensor_reduce(out=st[:, 0, :], in_=v3, axis=AX.X, op=ALU.add), reads=[vtk], writes=["st"])
            S.op("act", lambda e: e.activation(out=tqv, in_=vt, func=AF.Square), reads=[vtk], writes=[tqk])
            S.op("dve", lambda e: e.tensor_reduce(out=st[:, 1, :], in_=q3, axis=AX.X, op=ALU.add), reads=[tqk, "st"], writes=["st"])
            S.op("dve", lambda e: e.tensor_scalar(out=st[:, 2, :], in0=st[:, 0, :], scalar1=1.0 / 128, scalar2=None, op0=ALU.mult),
                 reads=["st"], writes=["st"])
            S.op("dve", lambda e: e.tensor_tensor(out=st[:, 3, :], in0=st[:, 2, :], in1=st[:, 2, :], op=ALU.mult),
                 reads=["st"], writes=["st"])
            S.op("dve", lambda e: e.tensor_scalar(out=st[:, 4, :], in0=st[:, 1, :], scalar1=1.0 / 128, scalar2=LN_EPS,
                                                  op0=ALU.mult, op1=ALU.add), reads=["st"], writes=["st"])
            S.op("dve", lambda e: e.tensor_tensor(out=st[:, 4, :], in0=st[:, 4, :], in1=st[:, 3, :], op=ALU.subtract),
                 reads=["st"], writes=["st"])
            S.op("act", lambda e: e.activation(out=st[:, 5, :], in_=st[:, 4, :], func=AF.Sqrt), reads=["st"], writes=["st"])
            S.op("dve", lambda e: e.reciprocal(out=st[:, 6, :], in_=st[:, 5, :]), reads=["st"], writes=["st"])
            mb = st[:, 2, :].unsqueeze(2).to_broadcast([128, NH, 128])
            rb = st[:, 6, :].unsqueeze(2).to_broadcast([128, NH, 128])
            S.op("dve", lambda e: e.tensor_tensor(out=v3, in0=v3, in1=mb, op=ALU.subtract), reads=[vtk, "st"], writes=[vtk])
            S.op("dve", lambda e: e.tensor_tensor(out=v3, in0=v3, in1=rb, op=ALU.mult), reads=[vtk, "st"], writes=[vtk])
            S.op("dve", lambda e: e.tensor_tensor(out=vt, in0=vt, in1=self.gvt[:], op=ALU.mult), reads=[vtk, "gvt"], writes=[vtk])
            S.op("dve", lambda e: e.tensor_tensor(out=nb, in0=vt, in1=self.bvt[:], op=ALU.add), reads=[vtk, "bvt"], writes=[nbk])

        def SGU(tc):
            banks = vbanks(tc)
            nb, nbk = self.nbf(tc % 2)
            for p in range(PA):
                bank = banks[p]
                S.op("pe", lambda e: e.matmul(self.ps[bank][:, 0:512], self.ones_pad[:, :],
                                              self.brow[:, DA + p * 512:DA + (p + 1) * 512], start=True, stop=False),
                     reads=["ones_pad", "rows"], writes=[("ps", bank)], milestone=False)
                for hh in range(4):
                    h = 4 * p + hh
                    oc = hh * 128
                    S.op("pe", lambda e: e.matmul(self.ps[bank][:, oc:oc + 128], nb[:, h * 128:(h + 1) * 128],
                                                  self.wT[:, h * 128:(h + 1) * 128], start=False, stop=(hh == 3)),
                         reads=[nbk, "wT"], writes=[("ps", bank)], milestone=(hh == 3))
            for p in range(PA):
                bank = banks[p]
                for hh in range(4):
                    h = 4 * p + hh
                    uv, uk = self.uT(h, tc * 128, (tc + 1) * 128)
                    yv, yk = self.yT(h, tc * 128, (tc + 1) * 128)
                    S.op("dve", lambda e: e.tensor_tensor(out=yv, in0=self.ps[bank][:, hh * 128:(hh + 1) * 128], in1=uv, op=ALU.mult),
                         reads=[("ps", bank), uk], writes=[yk])

        base = 2 * DA

        def convC(gp):
            pc = self.load_panel(win[:, base + c.DB + gp * 512:base + c.DB + (gp + 1) * 512], KD, 512)
            for jj in range(4):
                g = 4 * gp + jj
                bank = nextbank()
                self.zchunk_mm(pc[0], pc[1], jj, bank)
                cv, ck = self.cT(jj)
                S.op("act", lambda e: e.activation(out=cv, in_=self.ps[bank][:, 0:T], func=AF.Identity, bias=self.col("bC", g),
                                                   scale=1.0), reads=[("ps", bank), "cpp"], writes=[ck])
                if halo:
                    bh = nextbank()
                    for k in range(KD):
                        S.op("pe", lambda e: e.matmul(self.ps[bh][:, 0:2], pc[0][:, k, jj * 128:(jj + 1) * 128], self.act_h[:, k, :],
                                                      start=(k == 0), stop=(k == KD - 1)),
                             reads=[pc[1], "act_h"], writes=[("ps", bh)], milestone=(k == KD - 1))
                    S.op("act", lambda e: e.activation(out=self.ch[:, jj, :], in_=self.ps[bh][:, 0:2], func=AF.Identity,
                                                       bias=self.col("bC", g), scale=1.0),
                         reads=[("ps", bh), "cpp"], writes=[("ch", jj)])

        def convX(gp):
            px = self.load_panel(win[:, base + 2 * c.DB + gp * 512:base + 2 * c.DB + (gp + 1) * 512], KD, 512)
            for jj in range(4):
                g = 4 * gp + jj
                if halo:
                    bh = nextbank()
                    for k in range(KD):
                        S.op("pe", lambda e: e.matmul(self.ps[bh][:, 0:2], px[0][:, k, jj * 128:(jj + 1) * 128], self.act_h[:, k, :],
                                                      start=(k == 0), stop=(k == KD - 1)),
                             reads=[px[1], "act_h"], writes=[("ps", bh)], milestone=(k == KD - 1))
                    S.op("dve", lambda e: e.scalar_tensor_tensor(out=self.thx[:, jj % 2, :], in0=self.ps[bh][:, 0:2],
                                                                 scalar=self.col("bX", g), in1=self.ch[:, jj, :],
                                                                 op0=ALU.add, op1=ALU.mult),
                         reads=[("ps", bh), ("ch", jj), "cpp"], writes=[("thx", jj % 2)])
                    S.op("dve", lambda e: e.tensor_scalar(out=self.carry[:, g, :], in0=self.thx[:, jj % 2, :],
                                                          scalar1=self.col("hm"), scalar2=None, op0=ALU.mult),
                         reads=[("thx", jj % 2), "cpp"], writes=[("carry", g)])
                bank = nextbank()
                self.zchunk_mm(px[0], px[1], jj, bank)
                cv, ck = self.cT(jj)
                hv, hk = self.hc(jj % 2)
                S.op("dve", lambda e: e.tensor_copy(out=hv[:, 0:2], in_=self.carry[:, g, :]), reads=[("carry", g)], writes=[hk])
                S.op("dve", lambda e: e.scalar_tensor_tensor(out=hv[:, 2:T + 2], in0=self.ps[bank][:, 0:T], scalar=self.col("bX", g),
                                                             in1=cv, op0=ALU.add, op1=ALU.mult),
                     reads=[("ps", bank), ck, "cpp"], writes=[hk])
                S.op("dve", lambda e: e.tensor_copy(out=self.carry[:, g, :], in_=hv[:, T:T + 2]), reads=[hk], writes=[("carry", g)])
                ts = 4 + jj % 2
                acc = self.tf[ts]
                S.op("dve", lambda e: e.tensor_scalar(out=acc[:], in0=hv[:, 0:T], scalar1=self.col("cw", 0 * NG + g), scalar2=None,
                                                      op0=ALU.mult), reads=[hk, "cpp"], writes=[("tf", ts)])
                S.op("dve", lambda e: e.scalar_tensor_tensor(out=acc[:], in0=hv[:, 1:T + 1], scalar=self.col("cw", 1 * NG + g),
                                                             in1=acc[:], op0=ALU.mult, op1=ALU.add),
                     reads=[hk, "cpp", ("tf", ts)], writes=[("tf", ts)])
                S.op("dve", lambda e: e.scalar_tensor_tensor(out=cv, in0=hv[:, 2:T + 2], scalar=self.col("cw", 2 * NG + g),
                                                             in1=acc[:], op0=ALU.mult, op1=ALU.add),
                     reads=[hk, "cpp", ("tf", ts)], writes=[ck])

        def convB(gp):
            pb = self.load_panel(win[:, base + gp * 512:base + (gp + 1) * 512], KD, 512)
            for jj in range(4):
                g = 4 * gp + jj
                bank = nextbank()
                self.zchunk_mm(pb[0], pb[1], jj, bank)
                cv, ck = self.cT(jj)
                yv, yk = self.yT(NH + g)
                S.op("dve", lambda e: e.scalar_tensor_tensor(out=yv, in0=self.ps[bank][:, 0:T], scalar=self.col("bB", g), in1=cv,
                                                             op0=ALU.add, op1=ALU.mult),
                     reads=[("ps", bank), ck, "cpp"], writes=[yk])

        items = []
        for gp in range(PA):
            items += [lambda gp=gp: convC(gp), lambda gp=gp: convX(gp), lambda gp=gp: convB(gp)]

        def item():
            if items:
                items.pop(0)()

        assert TC == 4
        V(0); chain(0)
        V(1); chain(1)
        item()
        SGU(0)
        if FILL_V:
            self.fillers(FILL_V, nextbank())
        V(2); chain(2)
        SGU(1)
        if FILL_V:
            self.fillers(FILL_V, nextbank())
        V(3); chain(3)
        item()
        SGU(2)
        item()
        SGU(3)
        while items:
            item()

        wo = self.w_out.ap()
        self.stat_first = True
        pending = []
        for cg in range(c.CG):
            pv, pk = self.load_panel(wo[:, cg * 512:(cg + 1) * 512], KD, 512)
            for c4 in range(4):
                cc = 4 * cg + c4
                bank = nextbank()
                for k in range(KD):
                    yv, yk = self.yT(k)
                    S.op("pe", lambda e: e.matmul(self.ps[bank][:, 0:T], pv[:, k, c4 * 128:(c4 + 1) * 128], yv,
                                                  start=(k == 0), stop=(k == KD - 1)),
                         reads=[pk, yk], writes=[("ps", bank)], milestone=(k == KD - 1))
                while pending:
                    self.stat_mm(pending.pop(0), 4, 5)
                S.op("dve", lambda e: e.scalar_tensor_tensor(out=self.res[:, cc, :], in0=self.ps[bank][:, 0:T],
                                                             scalar=self.col("bout", cc), in1=self.res[:, cc, :],
                                                             op0=ALU.add, op1=ALU.add),
                     reads=[("ps", bank), ("res", cc), "cpp"], writes=[("res", cc)])
                self.stat_prep(cc)
                pending.append(cc)
        while pending:
            self.stat_mm(pending.pop(0), 4, 5)
        if FILL_LN:
            self.fillers(FILL_LN, 7)

    def res_load(self):
        c, S = self.cfg, self.S
        src = self.xT.ap()[:, self.cur_t0:self.cur_t0 + c.T].rearrange("(k p) n -> p k n", p=128)
        S.dma("sp", self.res[:], src, "xres", writes=[("res", k) for k in range(c.KD)])

    def build(self):
        c, S = self.cfg, self.S
        T, KD = c.T, c.KD
        self.deferred = []
        self.late_done = False
        self.stat_slot = {}
        self.setup()
        for ti in range(c.NT):
            t0 = ti * T
            self.cur_t0 = t0
            self.pid = 0
            self.conv_mod = (SCR_MOD, 0) if (ti == 0 and SCR_MOD) else None
            self.scr_new = []
            src = self.xT.ap()[:, t0:t0 + T].rearrange("(k p) n -> p k n", p=128)
            S.dma("pool", self.act[:], src, "xact", writes=[("act", k) for k in range(KD)])
            halo = (ti == 0 and self.use_halo)
            self.res_state = 0
            if not self.deferred:
                self.res_load()
                self.res_state = 1
                self.res_m = -4
            self.ffn("a", scale_res=True, halo=halo, fill=FILL_LN, first_tile=(ti == 0))
            self.layernorm(0, 1, "lnag", "lnab", False, t0)
            self.mixer(ti, halo=halo)
            self.layernorm(4, 5, "lnmg", "lnmb", False, t0)
            self.ffn("c", scale_res=False, kouter_first=True)
            self.layernorm(0, 1, "lncg", "lncb", True, t0)
            assert self.pid == self.NPAN, (self.pid, self.NPAN)
            self.scr_valid.update(self.scr_new)
        while self.deferred:
            self.deferred.pop(0)()
        S.wait_all("sp", [("res", cc) for cc in range(KD)])
        return self.nc


def host_inputs(cfg, inputs, core):
    c = cfg
    f = lambda a: np.ascontiguousarray(np.asarray(a, dtype=np.float32))
    x = np.asarray(inputs["x"], dtype=np.float32).reshape(c.BATCH * c.SEQ, c.D)
    r0 = core * c.NTOK
    m = {}
    m["xT"] = f(x[r0:r0 + c.NTOK].T)
    has_halo = (r0 % c.SEQ) != 0
    m["xh"] = f(x[r0 - 2:r0].T) if has_halo else np.zeros((c.D, 2), np.float32)
    m["wg_a"] = f(inputs["ffa_gate"][0]); m["wu_a"] = f(inputs["ffa_up"][0]); m["wd_a"] = f(inputs["ffa_down"][0])
    m["wg_c"] = f(inputs["ffc_gate"][0]); m["wu_c"] = f(inputs["ffc_up"][0]); m["wd_c"] = f(inputs["ffc_down"][0])
    m["w_in"] = f(inputs["w_in"][0]); m["w_out"] = f(inputs["w_out"][0])
    pp = lambda v: np.asarray(v, np.float32).reshape(-1, 128).T
    b_in = np.asarray(inputs["b_in"][0], np.float32)
    DA, DB = c.DA, c.DB
    cols = [pp(inputs["ln_a_g"][0]), pp(inputs["ln_a_b"][0]), pp(inputs["ln_m_g"][0]), pp(inputs["ln_m_b"][0]),
            pp(inputs["ln_c_g"][0]), pp(inputs["ln_c_b"][0]), pp(inputs["b_out"][0]),
            pp(b_in[0:DA]), pp(b_in[2 * DA:2 * DA + DB]), pp(b_in[2 * DA + DB:2 * DA + 2 * DB]), pp(b_in[2 * DA + 2 * DB:]),
            pp(np.asarray(inputs["conv_w"][0], np.float32)[0]), pp(np.asarray(inputs["conv_w"][0], np.float32)[1]),
            pp(np.asarray(inputs["conv_w"][0], np.float32)[2]),
            np.full((128, 1), 1.0 if has_halo else 0.0, np.float32)]
    m["cpp"] = f(np.concatenate(cols, axis=1))
    assert m["cpp"].shape == (128, c.NCC)
    m["rows"] = f(np.concatenate([b_in[DA:2 * DA], np.asarray(inputs["b_s"][0], np.float32).reshape(-1)])[None, :])
    m["gvbv"] = f(np.stack([np.asarray(inputs["ln_v_g"][0], np.float32), np.asarray(inputs["ln_v_b"][0], np.float32)]))
    ws = np.asarray(inputs["w_s"][0], np.float32)
    m["wsT"] = f(np.transpose(ws, (2, 0, 1)).reshape(128, c.NH * 128))
    return m


_CACHE = {}


def run(cfg, inputs, use_halo=True, trace=False):
    key = (cfg.D, cfg.F, cfg.SEQ, cfg.BATCH, cfg.NCORES, cfg.T, use_halo)
    if key not in _CACHE:
        _CACHE[key] = Prog(cfg, use_halo).build()
    nc = _CACHE[key]
    in_maps = [host_inputs(cfg, inputs, core) for core in range(cfg.NCORES)]
    res = run_bass_kernel_spmd(nc, in_maps, core_ids=list(range(cfg.NCORES)), **({"trace": True} if trace else {}))
    out = np.empty((cfg.BATCH * cfg.SEQ, cfg.D), np.float32)
    for core in range(cfg.NCORES):
        out[core * cfg.NTOK:(core + 1) * cfg.NTOK] = res.results[core]["outT"].T
    return out.reshape(cfg.BATCH, cfg.SEQ, cfg.D), res


def kernel(**inputs):
    cfg = Cfg()
    out, _ = run(cfg, inputs)
    return out
```

```python
import numpy as np
import concourse.bass as bass
import concourse.mybir as mybir
from concourse.bass_utils import run_bass_kernel_spmd

F32 = mybir.dt.float32
BF16 = mybir.dt.bfloat16
AF = mybir.ActivationFunctionType
ALU = mybir.AluOpType
AX = mybir.AxisListType

ALPHA = float(2.0 ** 0.25)
LN_EPS = 1e-5
GELU_C = 0.7978845608028654
FILL_LN = 36
FILL_V = 40
SCR_MOD = 3


class Cfg:
    def __init__(self, D=2048, F=5632, SEQ=8192, BATCH=2, NCORES=8, T=512):
        self.D, self.F, self.SEQ, self.BATCH, self.NCORES, self.T = D, F, SEQ, BATCH, NCORES, T
        self.DA = D // 2
        self.DB = D - self.DA
        self.NH = self.DA // 128
        self.NG = self.DB // 128
        self.DIN = 2 * self.DA + 3 * self.DB
        self.KD = D // 128
        self.KF = F // 128
        self.NTOK = BATCH * SEQ // NCORES
        self.NT = self.NTOK // T
        self.TC = T // 128
        self.KG = 4
        self.KPG = self.KF // 4
        self.CG = self.KD // 4
        self.FG = F // 512
        self.PA = self.DA // 512
        assert F % 512 == 0 and self.KF % 4 == 0 and self.KD % 4 == 0 and self.DA % 512 == 0
        assert self.KPG * 512 <= 8192 and self.KD * 512 <= 8192
        assert self.NTOK % T == 0 and SEQ % self.NTOK == 0 or self.NTOK % SEQ == 0
        off = {}
        c = 0
        for nm in ("lnag", "lnab", "lnmg", "lnmb", "lncg", "lncb", "bout"):
            off[nm] = c
            c += self.KD
        for nm, n in (("bu", self.NH), ("bB", self.NG), ("bC", self.NG), ("bX", self.NG), ("cw", 3 * self.NG), ("hm", 1)):
            off[nm] = c
            c += n
        self.off = off
        self.NCC = c


class Sched:
    def __init__(self, nc):
        self.nc = nc
        self.eng = {"pe": nc.tensor, "act": nc.scalar, "dve": nc.vector, "pool": nc.gpsimd, "sp": nc.sync}
        self.prog = {e: nc.alloc_semaphore(f"prog_{e}") for e in ("pe", "act", "dve", "pool")}
        self.cnt = {e: 0 for e in self.prog}
        self.waited = {e: {} for e in self.eng}
        self.lastw = {}
        self.readers = {}
        self.ivw = []
        self.ivr = {}
        self.dma_sems = {}
        self.nwaits = 0
        self.nops = 0

    @staticmethod
    def _is_iv(k):
        return isinstance(k, tuple) and len(k) == 4 and k[0] == "IV"

    def _deps(self, reads, writes):
        evs = []
        for k in reads:
            if self._is_iv(k):
                for ent in self.ivw:
                    if ent[0] == k[1] and ent[1] < k[3] and k[2] < ent[2]:
                        evs.append(ent[3])
            elif k in self.lastw:
                evs.append(self.lastw[k])
        for k in writes:
            if self._is_iv(k):
                for ent in self.ivw:
                    if ent[0] == k[1] and ent[1] < k[3] and k[2] < ent[2]:
                        evs.append(ent[3])
                for (nm, lo, hi, src), ev in self.ivr.items():
                    if nm == k[1] and lo < k[3] and k[2] < hi:
                        evs.append(ev)
            else:
                if k in self.lastw:
                    evs.append(self.lastw[k])
                evs.extend(self.readers.get(k, {}).values())
        return evs

    def _wait(self, e, evs):
        need = {}
        for (sem, val, src) in evs:
            if e == "pe" and src == "pe":
                continue
            key = id(sem)
            if self.waited[e].get(key, 0) >= val:
                continue
            if key not in need or need[key][1] < val:
                need[key] = (sem, val)
        for key, (sem, val) in need.items():
            self.eng[e].wait_ge(sem, val)
            self.waited[e][key] = val
            self.nwaits += 1

    def _record(self, ev, reads, writes):
        for k in writes:
            if self._is_iv(k):
                self.ivw = [ent for ent in self.ivw if not (ent[0] == k[1] and k[2] <= ent[1] and ent[2] <= k[3])]
                for rk in [rk for rk in self.ivr if rk[0] == k[1] and k[2] <= rk[1] and rk[2] <= k[3]]:
                    del self.ivr[rk]
                self.ivw.append([k[1], k[2], k[3], ev])
            else:
                self.lastw[k] = ev
                self.readers[k] = {}
        for k in reads:
            if k in writes:
                continue
            if self._is_iv(k):
                self.ivr[(k[1], k[2], k[3], ev[2] if ev[2] != "dma" else id(ev[0]))] = ev
            else:
                self.readers.setdefault(k, {})[ev[2] if ev[2] != "dma" else id(ev[0])] = ev

    def op(self, e, fn, reads=(), writes=(), milestone=True):
        self._wait(e, self._deps(reads, writes))
        ins = fn(self.eng[e])
        self.nops += 1
        if milestone:
            ins.then_inc(self.prog[e], 1)
            self.cnt[e] += 1
            ev = (self.prog[e], self.cnt[e], e)
        else:
            ev = (self.prog[e], self.cnt[e] + 1, e)
        self._record(ev, reads, writes)
        return ins

    def dma(self, e, out, in_, semname, reads=(), writes=()):
        self._wait(e, self._deps(reads, writes))
        if semname not in self.dma_sems:
            self.dma_sems[semname] = [self.nc.alloc_semaphore(f"dma_{semname}"), 0]
        ent = self.dma_sems[semname]
        ent[1] += 16
        self.eng[e].dma_start(out=out, in_=in_).then_inc(ent[0], 16)
        ev = (ent[0], ent[1], "dma")
        self._record(ev, reads, writes)
        return ev

    def wait_all(self, e, keys):
        evs = []
        for k in keys:
            if k in self.lastw:
                evs.append(self.lastw[k])
            evs.extend(self.readers.get(k, {}).values())
        self._wait(e, evs)


def IV(name, lo, hi):
    return ("IV", name, lo, hi)


class Prog:
    def __init__(self, cfg, use_halo=True):
        self.cfg = cfg
        self.use_halo = use_halo
        c = cfg
        nc = self.nc = bass.Bass("TRN2", target_bir_lowering=False)
        T, KD, KF, NH, NG = c.T, c.KD, c.KF, c.NH, c.NG
        dt = lambda n, s: nc.dram_tensor(n, s, F32, kind="ExternalInput")
        self.xT = dt("xT", [c.D, c.NTOK])
        self.xh = dt("xh", [c.D, 2])
        self.wg = {"a": dt("wg_a", [c.D, c.F]), "c": dt("wg_c", [c.D, c.F])}
        self.wu = {"a": dt("wu_a", [c.D, c.F]), "c": dt("wu_c", [c.D, c.F])}
        self.wd = {"a": dt("wd_a", [c.F, c.D]), "c": dt("wd_c", [c.F, c.D])}
        self.w_in = dt("w_in", [c.D, c.DIN])
        self.w_out = dt("w_out", [c.D, c.D])
        self.cpp_d = dt("cpp", [128, c.NCC])
        self.rows_d = dt("rows", [1, 2 * c.DA])
        self.gvbv_d = dt("gvbv", [2, c.DA])
        self.wsT_d = dt("wsT", [128, NH * 128])
        self.outT = nc.dram_tensor("outT", [c.D, c.NTOK], F32, kind="ExternalOutput")

        sb = nc.alloc_sbuf_tensor
        self.res = sb("res", [128, KD, T], F32)
        self.act = sb("act", [128, KD, T], BF16)
        self.OFF_Y = NH * T * 2
        self.OFF_X = self.OFF_Y + KD * T
        xsz = 8 * T + 2 * 2 * (T + 2)
        self.HSZ = max(KF * T, self.OFF_X + xsz)
        self.hreg = sb("hreg", [128, self.HSZ], BF16)
        self.NB = 4
        self.pan = [sb(f"pan{i}", [128, 8192], BF16) for i in range(self.NB)]
        self.tf = [sb(f"tf{i}", [128, T], F32) for i in range(6)]
        self.NRS = 3
        self.rbsq = sb("rbsq", [128, self.NRS, 2, T], BF16)
        self.cpp = sb("cppsb", [128, c.NCC], F32)
        self.cpa = sb("cpasb", [128, 4 * KD], F32)
        self.rstg = sb("rstg", [33, c.DA], F32)
        self.brow = sb("brow", [128, 2 * c.DA], BF16)
        self.ones_pad = sb("ones_pad", [128, 128], BF16)
        self.gvt = sb("gvt", [128, c.DA], F32)
        self.bvt = sb("bvt", [128, c.DA], F32)
        self.wT = sb("wTbf", [128, NH * 128], BF16)
        self.ones_bf = sb("ones_bf", [128, 128], BF16)
        self.zeros_bf = sb("zeros_bf", [128, 512], BF16)
        self.carry = sb("carry", [128, NG, 2], F32)
        self.st = sb("vstat", [128, 8, NH], F32)
        self.vtb = sb("vtb", [128, c.DA], F32)
        self.tqb = sb("tqb", [128, c.DA], F32)
        self.nbb = sb("nbb", [128, 2, c.DA], BF16)
        self.res_h = sb("res_h", [128, KD, 2], F32)
        self.act_h = sb("act_h", [128, KD, 2], BF16)
        self.hid_h = sb("hid_h", [128, KF, 2], BF16)
        self.sgh = sb("sgh", [128, 2, 2], F32)
        self.th = sb("th", [128, KD, 2], F32)
        self.rbsq_h = sb("rbsq_h", [128, 2, KD, 2], BF16)
        self.lnh = sb("lnh", [128, 4, 2], F32)
        self.ch = sb("ch", [128, 4, 2], F32)
        self.thx = sb("thx", [128, 2, 2], F32)
        self.ps = [nc.alloc_psum_tensor(f"ps{i}", [128, 512], F32) for i in range(8)]
        self.S = Sched(nc)
        self.pan_i = 0
        self.rs_i = 0
        self.NPAN = 2 * (2 * c.FG + c.CG * c.KG) + 5 * c.PA + c.CG
        self.use_scr = c.NT >= 2
        self.scr = nc.dram_tensor("wscr", [self.NPAN, 128, 8192], BF16) if self.use_scr else None
        self.scr_valid = set()

    def hid(self, m):
        T = self.cfg.T
        return self.hreg[:, m * T:(m + 1) * T], IV("h", m * T, (m + 1) * T)

    def uT(self, h, lo=0, hi=None):
        T = self.cfg.T
        hi = T if hi is None else hi
        v = self.hreg[:, 0:self.OFF_Y].bitcast(F32)
        return v[:, h * T + lo:h * T + hi], IV("h", 2 * (h * T + lo), 2 * (h * T + hi))

    def yT(self, cc, lo=0, hi=None):
        T = self.cfg.T
        hi = T if hi is None else hi
        o = self.OFF_Y + cc * T
        return self.hreg[:, o + lo:o + hi], IV("h", o + lo, o + hi)

    def xreg_f32(self, off_f32, n):
        o = self.OFF_X + 2 * off_f32
        return self.hreg[:, o:o + 2 * n].bitcast(F32), IV("h", o, o + 2 * n)

    def xreg_bf(self, off_bf, n):
        o = self.OFF_X + off_bf
        return self.hreg[:, o:o + n], IV("h", o, o + n)

    def cT(self, j):
        return self.xreg_f32(j * self.cfg.T, self.cfg.T)

    def hc(self, s):
        T = self.cfg.T
        return self.xreg_f32(4 * T + s * (T + 2), T + 2)

    def vt(self):
        return self.vtb[:], "vt"

    def tq(self):
        return self.tqb[:], "tq"

    def nbf(self, s):
        return self.nbb[:, s, :], ("nb", s)

    def col(self, name, i=0):
        o = self.cfg.off[name] + i
        return self.cpp[:, o:o + 1]

    def load_panel(self, dram_ap, kk, ncols, noscr=False):
        s = self.pan_i % self.NB
        self.pan_i += 1
        if noscr:
            pid = -1
        else:
            pid = self.pid
            self.pid += 1
        n = kk * ncols
        flat = self.pan[s][:, 0:n]
        view = flat.rearrange("p (k n) -> p k n", k=kk)
        if self.scr is not None and pid in self.scr_valid:
            self.S.dma("pool", flat, self.scr.ap()[pid][:, 0:n], f"pan{s}", reads=[("scr", pid)], writes=[("pan", s)])
        else:
            src = dram_ap.rearrange("(k p) n -> p k n", p=128)
            self.S.dma("pool", view, src, f"pan{s}", writes=[("pan", s)])
            if self.scr is not None and pid >= 0 and self.conv_mod is not None and pid % self.conv_mod[0] == self.conv_mod[1]:
                self.S.dma("sp", self.scr.ap()[pid][:, 0:n], flat, f"st{s}", reads=[("pan", s)], writes=[("scr", pid)])
                self.scr_new.append(pid)
        return view, ("pan", s)

    def setup(self):
        c, S, nc = self.cfg, self.S, self.nc
        S.dma("sp", self.cpp[:], self.cpp_d.ap(), "c_cpp", writes=["cpp"])
        S.op("pool", lambda e: e.memset(self.ones_bf[:], 1.0 / c.D), writes=["ones_bf"])
        S.op("pool", lambda e: e.memset(self.zeros_bf[:], 0.0), writes=["zeros_bf"])
        S.op("pool", lambda e: e.memset(self.carry[:], 0.0), writes=["carry"])

    def setup_late(self):
        c, S, nc = self.cfg, self.S, self.nc
        NH = c.NH
        S.op("pool", lambda e: e.memset(self.brow[:], 0.0), writes=["rows"])
        S.op("pool", lambda e: e.memset(self.ones_pad[:], 0.0), writes=["ones_pad"])
        S.op("pool", lambda e: e.memset(self.ones_pad[0:1, :], 1.0), reads=["ones_pad"], writes=["ones_pad"])
        S.op("pool", lambda e: e.memset(self.ones_pad[32:33, :], 1.0), reads=["ones_pad"], writes=["ones_pad"])
        for hh in range(2):
            seg = slice(hh * c.DA, (hh + 1) * c.DA)
            S.dma("sp", self.rstg[0:1, :], self.rows_d.ap()[0:1, seg], "c_rows0", writes=["rstg0"])
            S.dma("sp", self.rstg[32:33, :], self.rows_d.ap()[0:1, seg], "c_rows1", writes=["rstg1"])
            S.op("dve", lambda e: e.tensor_copy(out=self.brow[0:1, seg], in_=self.rstg[0:1, :]), reads=["rstg0", "rows"], writes=["rows"])
            S.op("dve", lambda e: e.tensor_copy(out=self.brow[32:33, seg], in_=self.rstg[32:33, :]), reads=["rstg1", "rows"], writes=["rows"])
            S.op("dve", lambda e: e.tensor_tensor(out=self.rstg[32:33, :], in0=self.rstg[32:33, :], in1=self.brow[32:33, seg],
                                                  op=ALU.subtract), reads=["rstg1", "rows"], writes=["rstg1"])
            S.op("dve", lambda e: e.tensor_copy(out=self.brow[32:33, seg], in_=self.rstg[32:33, :]), reads=["rstg1", "rows"], writes=["rows"])
        S.dma("sp", self.gvt[:], self.gvbv_d.ap()[0:1, :].partition_broadcast(128), "c_gv", writes=["gvt"])
        S.dma("sp", self.bvt[:], self.gvbv_d.ap()[1:2, :].partition_broadcast(128), "c_bv", writes=["bvt"])
        wst = self.vtb[:]
        S.dma("sp", wst, self.wsT_d.ap(), "c_ws", writes=["vt"])
        for h in range(NH):
            S.op("pool", lambda e: e.affine_select(out=wst[:, h * 128:(h + 1) * 128], in_=wst[:, h * 128:(h + 1) * 128],
                                                   pattern=[[1, 128]], compare_op=ALU.is_ge, fill=0.0, base=0,
                                                   channel_multiplier=-1), reads=["vt"], writes=["vt"])
        S.op("pool", lambda e: e.tensor_copy(out=self.wT[:], in_=wst), reads=["vt"], writes=["wT"])
        o = c.off
        S.op("dve", lambda e: e.tensor_scalar(out=self.cpa[:, 0:4 * c.KD], in0=self.cpp[:, o["lnag"]:o["lnag"] + 4 * c.KD],
                                              scalar1=ALPHA, scalar2=None, op0=ALU.mult), reads=["cpp"], writes=["cpa"])

    def gateup_evac(self, m, banks=None):
        S, T = self.S, self.cfg.T
        bg, bu = banks if banks is not None else (m % 2, 2 + m % 2)
        sg = self.tf[m % 2]
        S.op("act", lambda e: e.activation(out=sg[:], in_=self.ps[bg][:, 0:T], func=AF.Silu),
             reads=[("ps", bg)], writes=[("tf", m % 2)])
        hv, hk = self.hid(m)
        S.op("dve", lambda e: e.tensor_tensor(out=hv, in0=sg[:], in1=self.ps[bu][:, 0:T], op=ALU.mult),
             reads=[("tf", m % 2), ("ps", bu)], writes=[hk])

    def halo_load(self):
        S = self.S
        hsrc = self.xh.ap().rearrange("(k p) n -> p k n", p=128)
        S.dma("pool", self.act_h[:], hsrc, "xhact", writes=["act_h"])
        S.dma("sp", self.res_h[:], hsrc, "xhres", writes=["res_h"])

    def halo_gateup(self, m, pg, kg_, pu, ku_, cs):
        S, KD = self.S, self.cfg.KD
        bgh, buh = 4 + m % 2, 6 + m % 2
        for k in range(KD):
            S.op("pe", lambda e: e.matmul(self.ps[bgh][:, 0:2], pg[:, k, cs], self.act_h[:, k, :],
                                          start=(k == 0), stop=(k == KD - 1)),
                 reads=[kg_, "act_h"], writes=[("ps", bgh)], milestone=(k == KD - 1))
        for k in range(KD):
            S.op("pe", lambda e: e.matmul(self.ps[buh][:, 0:2], pu[:, k, cs], self.act_h[:, k, :],
                                          start=(k == 0), stop=(k == KD - 1)),
                 reads=[ku_, "act_h"], writes=[("ps", buh)], milestone=(k == KD - 1))
        S.op("act", lambda e: e.activation(out=self.sgh[:, m % 2, :], in_=self.ps[bgh][:, 0:2], func=AF.Silu),
             reads=[("ps", bgh)], writes=[("sgh", m % 2)])
        S.op("dve", lambda e: e.tensor_tensor(out=self.hid_h[:, m, :], in0=self.sgh[:, m % 2, :],
                                              in1=self.ps[buh][:, 0:2], op=ALU.mult),
             reads=[("sgh", m % 2), ("ps", buh)], writes=[("hid_h", m)])

    def after_mchunk(self, m):
        if self.deferred:
            self.deferred.pop(0)()
        elif self.res_state == 0:
            self.res_load()
            self.res_state = 1
            self.res_m = m

    def ffn(self, which, scale_res, halo=False, kouter_first=False, fill=0, first_tile=False):
        c, S = self.cfg, self.S
        T, KD = c.T, c.KD
        wg, wu, wd = self.wg[which].ap(), self.wu[which].ap(), self.wd[which].ap()
        for j in range(c.FG):
            if j == 0 and first_tile:
                for mm in range(4):
                    pgs, kgs = self.load_panel(wg[:, mm * 128:(mm + 1) * 128], KD, 128, noscr=True)
                    pus, kus = self.load_panel(wu[:, mm * 128:(mm + 1) * 128], KD, 128, noscr=True)
                    if mm == 0:
                        self.pid += 2
                    m = mm
                    bg, bu = m % 2, 2 + m % 2
                    for k in range(KD):
                        S.op("pe", lambda e: e.matmul(self.ps[bg][:, 0:T], pgs[:, k, :], self.act[:, k, :],
                                                      start=(k == 0), stop=(k == KD - 1)),
                             reads=[kgs, ("act", k)], writes=[("ps", bg)], milestone=(k == KD - 1))
                    for k in range(KD):
                        S.op("pe", lambda e: e.matmul(self.ps[bu][:, 0:T], pus[:, k, :], self.act[:, k, :],
                                                      start=(k == 0), stop=(k == KD - 1)),
                             reads=[kus, ("act", k)], writes=[("ps", bu)], milestone=(k == KD - 1))
                    self.gateup_evac(m)
                    if halo:
                        if mm == 0:
                            self.halo_load()
                        self.halo_gateup(m, pgs, kgs, pus, kus, slice(0, 128))
                    self.after_mchunk(m)
                if not self.late_done:
                    self.setup_late()
                    self.late_done = True
                if c.FG > 1:
                    continue
                pg = pu = None
                mm_list = []
            else:
                pg, kg_ = self.load_panel(wg[:, j * 512:(j + 1) * 512], KD, 512)
                pu, ku_ = self.load_panel(wu[:, j * 512:(j + 1) * 512], KD, 512)
                mm_list = list(range(4))
            if kouter_first and j == 0:
                for k in range(KD):
                    for mm in range(4):
                        for (pan, pk_, bank) in ((pg, kg_, mm), (pu, ku_, 4 + mm)):
                            S.op("pe", lambda e: e.matmul(self.ps[bank][:, 0:T], pan[:, k, mm * 128:(mm + 1) * 128], self.act[:, k, :],
                                                          start=(k == 0), stop=(k == KD - 1)),
                                 reads=[pk_, ("act", k)], writes=[("ps", bank)], milestone=(k == KD - 1))
                for mm in range(4):
                    self.gateup_evac(mm, banks=(mm, 4 + mm))
                    self.after_mchunk(mm)
                mm_list = []
            for mm in mm_list:
                m = 4 * j + mm
                bg, bu = m % 2, 2 + m % 2
                for k in range(KD):
                    S.op("pe", lambda e: e.matmul(self.ps[bg][:, 0:T], pg[:, k, mm * 128:(mm + 1) * 128], self.act[:, k, :],
                                                  start=(k == 0), stop=(k == KD - 1)),
                         reads=[kg_, ("act", k)], writes=[("ps", bg)], milestone=(k == KD - 1))
                for k in range(KD):
                    S.op("pe", lambda e: e.matmul(self.ps[bu][:, 0:T], pu[:, k, mm * 128:(mm + 1) * 128], self.act[:, k, :],
                                                  start=(k == 0), stop=(k == KD - 1)),
                         reads=[ku_, ("act", k)], writes=[("ps", bu)], milestone=(k == KD - 1))
                self.gateup_evac(m)
                if halo:
                    self.halo_gateup(m, pg, kg_, pu, ku_, slice(mm * 128, (mm + 1) * 128))
                self.after_mchunk(m)
            if j == 0 and not self.late_done:
                self.setup_late()
                self.late_done = True
            if j == c.FG - 1:
                while self.deferred:
                    self.deferred.pop(0)()
                if self.res_state == 0:
                    self.res_load()
                    self.res_state = 1
                    self.res_m = -100
            if scale_res and self.res_state == 1 and j == c.FG - 1:
                self.res_state = 2
                for cc in range(KD):
                    S.op("dve", lambda e: e.tensor_scalar(out=self.res[:, cc, :], in0=self.res[:, cc, :], scalar1=ALPHA,
                                                          scalar2=None, op0=ALU.mult),
                         reads=[("res", cc)], writes=[("res", cc)])
                if halo:
                    S.op("dve", lambda e: e.tensor_scalar(out=self.res_h[:], in0=self.res_h[:], scalar1=ALPHA, scalar2=None,
                                                          op0=ALU.mult), reads=["res_h"], writes=["res_h"])
        assert not self.deferred and (not scale_res or self.res_state == 2)
        self.stat_first = True
        self.hblk = 0
        pending = []

        def evac(cc, bank):
            S.op("dve", lambda e: e.scalar_tensor_tensor(out=self.res[:, cc, :], in0=self.ps[bank][:, 0:T], scalar=0.5,
                                                         in1=self.res[:, cc, :], op0=ALU.mult, op1=ALU.add),
                 reads=[("ps", bank), ("res", cc)], writes=[("res", cc)])
            self.stat_prep(cc)
            pending.append(cc)

        for cg in range(c.CG):
            for kg in range(c.KG):
                pd, kd_ = self.load_panel(wd[kg * c.KPG * 128:(kg + 1) * c.KPG * 128, cg * 512:(cg + 1) * 512], c.KPG, 512)
                for c4 in range(4):
                    cc = 4 * cg + c4
                    bank = 4 + c4
                    for kk in range(c.KPG):
                        k = kg * c.KPG + kk
                        hv, hk = self.hid(k)
                        S.op("pe", lambda e: e.matmul(self.ps[bank][:, 0:T], pd[:, kk, c4 * 128:(c4 + 1) * 128], hv,
                                                      start=(kg == 0 and kk == 0), stop=(kg == c.KG - 1 and kk == c.KPG - 1)),
                             reads=[kd_, hk], writes=[("ps", bank)], milestone=(kk == c.KPG - 1))
                    if halo:
                        bh = 2 + self.hblk % 2
                        self.hblk += 1
                        for kk in range(c.KPG):
                            k = kg * c.KPG + kk
                            S.op("pe", lambda e: e.matmul(self.ps[bh][:, 0:2], pd[:, kk, c4 * 128:(c4 + 1) * 128], self.hid_h[:, k, :],
                                                          start=(kk == 0), stop=(kk == c.KPG - 1)),
                                 reads=[kd_, ("hid_h", k)], writes=[("ps", bh)], milestone=(kk == c.KPG - 1))
                        S.op("dve", lambda e: e.scalar_tensor_tensor(out=self.res_h[:, cc, :], in0=self.ps[bh][:, 0:2], scalar=0.5,
                                                                     in1=self.res_h[:, cc, :], op0=ALU.mult, op1=ALU.add),
                             reads=[("ps", bh), "res_h"], writes=["res_h"])
                    while pending:
                        self.stat_mm(pending.pop(0), 0, 1)
                    if kg == c.KG - 1:
                        evac(cc, bank)
        while pending:
            self.stat_mm(pending.pop(0), 0, 1)
        if halo:
            self.halo_ln()
        if fill:
            self.fillers(fill, 3)

    def halo_ln(self):
        c, S = self.cfg, self.S
        KD = c.KD
        S.op("act", lambda e: e.activation(out=self.rbsq_h[:, 0, :, :], in_=self.res_h[:], func=AF.Copy),
             reads=["res_h"], writes=["rbh"])
        S.op("act", lambda e: e.activation(out=self.rbsq_h[:, 1, :, :], in_=self.res_h[:], func=AF.Square),
             reads=["res_h"], writes=["sqh"])
        for cc in range(KD):
            S.op("pe", lambda e: e.matmul(self.ps[2][:, 0:2], self.ones_bf[:], self.rbsq_h[:, 0, cc, :], start=(cc == 0),
                                          stop=(cc == KD - 1)), reads=["rbh", "ones_bf"], writes=[("ps", 2)], milestone=(cc == KD - 1))
        for cc in range(KD):
            S.op("pe", lambda e: e.matmul(self.ps[3][:, 0:2], self.ones_bf[:], self.rbsq_h[:, 1, cc, :], start=(cc == 0),
                                          stop=(cc == KD - 1)), reads=["sqh", "ones_bf"], writes=[("ps", 3)], milestone=(cc == KD - 1))
        mean, var, rstd = self.lnh[:, 0, :], self.lnh[:, 1, :], self.lnh[:, 2, :]
        S.op("dve", lambda e: e.tensor_copy(out=mean, in_=self.ps[2][:, 0:2]), reads=[("ps", 2)], writes=["lnh"])
        S.op("dve", lambda e: e.tensor_tensor(out=var, in0=mean, in1=mean, op=ALU.mult), reads=["lnh"], writes=["lnh"])
        S.op("dve", lambda e: e.scalar_tensor_tensor(out=var, in0=self.ps[3][:, 0:2], scalar=LN_EPS, in1=var, op0=ALU.add,
                                                     op1=ALU.subtract), reads=[("ps", 3), "lnh"], writes=["lnh"])
        S.op("act", lambda e: e.activation(out=var, in_=var, func=AF.Sqrt), reads=["lnh"], writes=["lnh"])
        S.op("dve", lambda e: e.reciprocal(out=rstd, in_=var), reads=["lnh"], writes=["lnh"])
        mb = mean.unsqueeze(1).to_broadcast([128, KD, 2])
        rb = rstd.unsqueeze(1).to_broadcast([128, KD, 2])
        go, bo = c.off["lnag"], c.off["lnab"]
        gb = self.cpp[:, go:go + KD].unsqueeze(2).to_broadcast([128, KD, 2])
        bb = self.cpp[:, bo:bo + KD].unsqueeze(2).to_broadcast([128, KD, 2])
        S.op("dve", lambda e: e.tensor_tensor(out=self.th[:], in0=self.res_h[:], in1=mb, op=ALU.subtract),
             reads=["res_h", "lnh"], writes=["th"])
        S.op("dve", lambda e: e.tensor_tensor(out=self.th[:], in0=self.th[:], in1=rb, op=ALU.mult), reads=["th", "lnh"], writes=["th"])
        S.op("dve", lambda e: e.tensor_tensor(out=self.th[:], in0=self.th[:], in1=gb, op=ALU.mult), reads=["th", "cpp"], writes=["th"])
        S.op("dve", lambda e: e.tensor_tensor(out=self.act_h[:], in0=self.th[:], in1=bb, op=ALU.add), reads=["th", "cpp"],
             writes=["act_h"])

    def stat_prep(self, cc):
        S, T = self.S, self.cfg.T
        s = self.rs_i % self.NRS
        self.rs_i += 1
        S.op("act", lambda e: e.activation(out=self.rbsq[:, s, 0, :], in_=self.res[:, cc, :], func=AF.Copy),
             reads=[("res", cc)], writes=[("rb", s)])
        S.op("act", lambda e: e.activation(out=self.rbsq[:, s, 1, :], in_=self.res[:, cc, :], func=AF.Square),
             reads=[("res", cc)], writes=[("sq", s)])
        self.stat_slot[cc] = s

    def stat_mm(self, cc, b1, b2):
        S, T, KD = self.S, self.cfg.T, self.cfg.KD
        s = self.stat_slot[cc]
        first = self.stat_first
        self.stat_n = 1 if first else self.stat_n + 1
        self.stat_first = False
        last = (self.stat_n == KD)
        S.op("pe", lambda e: e.matmul(self.ps[b1][:, 0:T], self.ones_bf[:], self.rbsq[:, s, 0, :], start=first, stop=last),
             reads=[("rb", s), "ones_bf"], writes=[("ps", b1)], milestone=True)
        S.op("pe", lambda e: e.matmul(self.ps[b2][:, 0:T], self.ones_bf[:], self.rbsq[:, s, 1, :], start=first, stop=last),
             reads=[("sq", s), "ones_bf"], writes=[("ps", b2)], milestone=True)

    def fillers(self, n, bank):
        S = self.S
        for i in range(n):
            S.op("pe", lambda e: e.matmul(self.ps[bank][:, 0:512], self.zeros_bf[:, 0:128], self.zeros_bf[:, :], start=True, stop=True),
                 reads=["zeros_bf"], writes=[("ps", bank)], milestone=(i == n - 1))

    def layernorm(self, b1, b2, gname, bname, final, t0):
        c, S = self.cfg, self.S
        T, KD = c.T, c.KD
        mean, var, rstd = self.tf[2], self.tf[3], self.tf[4]
        S.op("dve", lambda e: e.tensor_copy(out=mean[:], in_=self.ps[b1][:, 0:T]), reads=[("ps", b1)], writes=[("tf", 2)])
        S.op("dve", lambda e: e.tensor_tensor(out=var[:], in0=mean[:], in1=mean[:], op=ALU.mult),
             reads=[("tf", 2)], writes=[("tf", 3)])
        S.op("dve", lambda e: e.scalar_tensor_tensor(out=var[:], in0=self.ps[b2][:, 0:T], scalar=LN_EPS, in1=var[:],
                                                     op0=ALU.add, op1=ALU.subtract),
             reads=[("ps", b2), ("tf", 3)], writes=[("tf", 3)])
        S.op("act", lambda e: e.activation(out=var[:], in_=var[:], func=AF.Sqrt), reads=[("tf", 3)], writes=[("tf", 3)])
        S.op("dve", lambda e: e.reciprocal(out=rstd[:], in_=var[:]), reads=[("tf", 3)], writes=[("tf", 4)])
        go, bo = c.off[gname], c.off[bname]
        ao = {"lnag": 0, "lnab": KD, "lnmg": 2 * KD, "lnmb": 3 * KD}
        tslots = (5, 3) if final else (0, 1)

        def apply(cc):
            ts = tslots[cc % 2]
            t = self.tf[ts]
            S.op("dve", lambda e: e.tensor_tensor(out=t[:], in0=self.res[:, cc, :], in1=mean[:], op=ALU.subtract),
                 reads=[("res", cc), ("tf", 2)], writes=[("tf", ts)])
            S.op("dve", lambda e: e.tensor_tensor(out=t[:], in0=t[:], in1=rstd[:], op=ALU.mult),
                 reads=[("tf", ts), ("tf", 4)], writes=[("tf", ts)])
            if final:
                S.op("act", lambda e: e.activation(out=self.res[:, cc, :], in_=t[:], func=AF.Identity,
                                                   bias=self.cpp[:, bo + cc:bo + cc + 1], scale=self.cpp[:, go + cc:go + cc + 1]),
                     reads=[("tf", ts), "cpp"], writes=[("res", cc)])
            else:
                ag, ab = ao[gname] + cc, ao[bname] + cc
                S.op("act", lambda e: e.activation(out=self.act[:, cc, :], in_=t[:], func=AF.Identity,
                                                   bias=self.cpp[:, bo + cc:bo + cc + 1], scale=self.cpp[:, go + cc:go + cc + 1]),
                     reads=[("tf", ts), "cpp"], writes=[("act", cc)])
                S.op("act", lambda e: e.activation(out=self.res[:, cc, :], in_=t[:], func=AF.Identity,
                                                   bias=self.cpa[:, ab:ab + 1], scale=self.cpa[:, ag:ag + 1]),
                     reads=[("tf", ts), "cpa"], writes=[("res", cc)])

        def store():
            dst = self.outT.ap()[:, t0:t0 + T].rearrange("(k p) n -> p k n", p=128)
            S.dma("sp", dst, self.res[:], "out", reads=[("res", cc) for cc in range(KD)])

        if final:
            for cc in range(KD):
                self.deferred.append(lambda cc=cc: apply(cc))
            self.deferred.append(store)
        else:
            for cc in range(KD):
                apply(cc)

    def zchunk_mm(self, pv, pk, jj, bank):
        S, T, KD = self.S, self.cfg.T, self.cfg.KD
        for k in range(KD):
            S.op("pe", lambda e: e.matmul(self.ps[bank][:, 0:T], pv[:, k, jj * 128:(jj + 1) * 128], self.act[:, k, :],
                                          start=(k == 0), stop=(k == KD - 1)),
                 reads=[pk, ("act", k)], writes=[("ps", bank)], milestone=(k == KD - 1))

    def mixer(self, tile_idx, halo=False):
        c, S = self.cfg, self.S
        T, KD, NH, NG, DA, PA, TC = c.T, c.KD, c.NH, c.NG, c.DA, c.PA, c.TC
        win = self.w_in.ap()
        self.zb = 0

        def nextbank():
            b = self.zb % 4
            self.zb += 1
            return b

        def u_evac(h, bank):
            xs, qs = h % 2, 2 + h % 2
            tx, tq_ = self.tf[xs], self.tf[qs]
            S.op("act", lambda e: e.activation(out=tx[:], in_=self.ps[bank][:, 0:T], func=AF.Identity,
                                               bias=self.col("bu", h), scale=1.0),
                 reads=[("ps", bank), "cpp"], writes=[("tf", xs)])
            S.op("act", lambda e: e.activation(out=tq_[:], in_=tx[:], func=AF.Square), reads=[("tf", xs)], writes=[("tf", qs)])
            S.op("dve", lambda e: e.tensor_scalar(out=tq_[:], in0=tq_[:], scalar1=0.044715, scalar2=1.0, op0=ALU.mult,
                                                  op1=ALU.add), reads=[("tf", qs)], writes=[("tf", qs)])
            S.op("dve", lambda e: e.tensor_tensor(out=tq_[:], in0=tq_[:], in1=tx[:], op=ALU.mult),
                 reads=[("tf", qs), ("tf", xs)], writes=[("tf", qs)])
            S.op("act", lambda e: e.activation(out=tq_[:], in_=tq_[:], func=AF.Sigmoid, scale=2.0 * GELU_C),
                 reads=[("tf", qs)], writes=[("tf", qs)])
            uv, uk = self.uT(h)
            S.op("dve", lambda e: e.tensor_tensor(out=uv, in0=tq_[:], in1=tx[:], op=ALU.mult),
                 reads=[("tf", qs), ("tf", xs)], writes=[uk])

        ups = [self.load_panel(win[:, p * 512:(p + 1) * 512], KD, 512) for p in range(PA)]
        for k in range(KD):
            for p in range(PA):
                pv, pk = ups[p]
                for jj in range(4):
                    bank = 4 * p + jj
                    S.op("pe", lambda e: e.matmul(self.ps[bank][:, 0:T], pv[:, k, jj * 128:(jj + 1) * 128], self.act[:, k, :],
                                                  start=(k == 0), stop=(k == KD - 1)),
                         reads=[pk, ("act", k)], writes=[("ps", bank)], milestone=(k == KD - 1))
        for h in range(4 * PA):
            u_evac(h, h)

        vp = [self.load_panel(win[:, DA + p * 512:DA + (p + 1) * 512], KD, 512) for p in range(PA)]
        vt, vtk = self.vt()
        tqv, tqk = self.tq()
        st = self.st

        def vbanks(tc):
            return [4 + 2 * (tc % 2) + p for p in range(PA)]

        def V(tc):
            for p in range(PA):
                pv, pk = vp[p]
                bank = vbanks(tc)[p]
                for k in range(KD):
                    S.op("pe", lambda e: e.matmul(self.ps[bank][:, 0:512], self.act[:, k, tc * 128:(tc + 1) * 128], pv[:, k, :],
                                                  start=(k == 0), stop=False),
                         reads=[pk, ("act", k)], writes=[("ps", bank)], milestone=False)
                S.op("pe", lambda e: e.matmul(self.ps[bank][:, 0:512], self.ones_pad[:, :],
                                              self.brow[:, p * 512:(p + 1) * 512], start=False, stop=True),
                     reads=["ones_pad", "rows"], writes=[("ps", bank)], milestone=True)

        def chain(tc):
            banks = vbanks(tc)
            nb, nbk = self.nbf(tc % 2)
            for p in range(PA):
                bank = banks[p]
                sl = slice(p * 512, (p + 1) * 512)
                S.op("act", lambda e: e.activation(out=tqv[:, sl], in_=self.ps[bank][:, 0:512], func=AF.Square),
                     reads=[("ps", bank)], writes=[tqk])
                S.op("dve", lambda e: e.tensor_scalar(out=tqv[:, sl], in0=tqv[:, sl], scalar1=0.044715, scalar2=1.0,
                                                      op0=ALU.mult, op1=ALU.add), reads=[tqk], writes=[tqk])
                S.op("dve", lambda e: e.tensor_tensor(out=tqv[:, sl], in0=tqv[:, sl], in1=self.ps[bank][:, 0:512], op=ALU.mult),
                     reads=[tqk, ("ps", bank)], writes=[tqk])
                S.op("act", lambda e: e.activation(out=tqv[:, sl], in_=tqv[:, sl], func=AF.Sigmoid, scale=2.0 * GELU_C),
                     reads=[tqk], writes=[tqk])
                S.op("dve", lambda e: e.tensor_tensor(out=vt[:, sl], in0=tqv[:, sl], in1=self.ps[bank][:, 0:512], op=ALU.mult),
                     reads=[tqk, ("ps", bank)], writes=[vtk])
            v3 = vt.rearrange("p (h d) -> p h d", h=NH)
            q3 = tqv.rearrange("p (h d) -> p h d", h=NH)
            S.op("dve", lambda e: e.tensor_reduce(out=st[:, 0, :], in_=v3, axis=AX.X, op=ALU.add), reads=[vtk], writes=["st"])
            S.op("act", lambda e: e.activation(out=tqv, in_=vt, func=AF.Square), reads=[vtk], writes=[tqk])
            S.op("dve", lambda e: e.tensor_reduce(out=st[:, 1, :], in_=q3, axis=AX.X, op=ALU.add), reads=[tqk, "st"], writes=["st"])
            S.op("dve", lambda e: e.tensor_scalar(out=st[:, 2, :], in0=st[:, 0, :], scalar1=1.0 / 128, scalar2=None, op0=ALU.mult),
                 reads=["st"], writes=["st"])
            S.op("dve", lambda e: e.tensor_tensor(out=st[:, 3, :], in0=st[:, 2, :], in1=st[:, 2, :], op=ALU.mult),
                 reads=["st"], writes=["st"])
            S.op("dve", lambda e: e.tensor_scalar(out=st[:, 4, :], in0=st[:, 1, :], scalar1=1.0 / 128, scalar2=LN_EPS,
                                                  op0=ALU.mult, op1=ALU.add), reads=["st"], writes=["st"])
            S.op("dve", lambda e: e.tensor_tensor(out=st[:, 4, :], in0=st[:, 4, :], in1=st[:, 3, :], op=ALU.subtract),
                 reads=["st"], writes=["st"])
            S.op("act", lambda e: e.activation(out=st[:, 5, :], in_=st[:, 4, :], func=AF.Sqrt), reads=["st"], writes=["st"])
            S.op("dve", lambda e: e.reciprocal(out=st[:, 6, :], in_=st[:, 5, :]), reads=["st"], writes=["st"])
            mb = st[:, 2, :].unsqueeze(2).to_broadcast([128, NH, 128])
            rb = st[:, 6, :].unsqueeze(2).to_broadcast([128, NH, 128])
            S.op("dve", lambda e: e.tensor_tensor(out=v3, in0=v3, in1=mb, op=ALU.subtract), reads=[vtk, "st"], writes=[vtk])
            S.op("dve", lambda e: e.tensor_tensor(out=v3, in0=v3, in1=rb, op=ALU.mult), reads=[vtk, "st"], writes=[vtk])
            S.op("dve", lambda e: e.tensor_tensor(out=vt, in0=vt, in1=self.gvt[:], op=ALU.mult), reads=[vtk, "gvt"], writes=[vtk])
            S.op("dve", lambda e: e.tensor_tensor(out=nb, in0=vt, in1=self.bvt[:], op=ALU.add), reads=[vtk, "bvt"], writes=[nbk])

        def SGU(tc):
            banks = vbanks(tc)
            nb, nbk = self.nbf(tc % 2)
            for p in range(PA):
                bank = banks[p]
                S.op("pe", lambda e: e.matmul(self.ps[bank][:, 0:512], self.ones_pad[:, :],
                                              self.brow[:, DA + p * 512:DA + (p + 1) * 512], start=True, stop=False),
                     reads=["ones_pad", "rows"], writes=[("ps", bank)], milestone=False)
                for hh in range(4):
                    h = 4 * p + hh
                    oc = hh * 128
                    S.op("pe", lambda e: e.matmul(self.ps[bank][:, oc:oc + 128], nb[:, h * 128:(h + 1) * 128],
                                                  self.wT[:, h * 128:(h + 1) * 128], start=False, stop=(hh == 3)),
                         reads=[nbk, "wT"], writes=[("ps", bank)], milestone=(hh == 3))
            for p in range(PA):
                bank = banks[p]
                for hh in range(4):
                    h = 4 * p + hh
                    uv, uk = self.uT(h, tc * 128, (tc + 1) * 128)
                    yv, yk = self.yT(h, tc * 128, (tc + 1) * 128)
                    S.op("dve", lambda e: e.tensor_tensor(out=yv, in0=self.ps[bank][:, hh * 128:(hh + 1) * 128], in1=uv, op=ALU.mult),
                         reads=[("ps", bank), uk], writes=[yk])

        base = 2 * DA

        def convC(gp):
            pc = self.load_panel(win[:, base + c.DB + gp * 512:base + c.DB + (gp + 1) * 512], KD, 512)
            for jj in range(4):
                g = 4 * gp + jj
                bank = nextbank()
                self.zchunk_mm(pc[0], pc[1], jj, bank)
                cv, ck = self.cT(jj)
                S.op("act", lambda e: e.activation(out=cv, in_=self.ps[bank][:, 0:T], func=AF.Identity, bias=self.col("bC", g),
                                                   scale=1.0), reads=[("ps", bank), "cpp"], writes=[ck])
                if halo:
                    bh = nextbank()
                    for k in range(KD):
                        S.op("pe", lambda e: e.matmul(self.ps[bh][:, 0:2], pc[0][:, k, jj * 128:(jj + 1) * 128], self.act_h[:, k, :],
                                                      start=(k == 0), stop=(k == KD - 1)),
                             reads=[pc[1], "act_h"], writes=[("ps", bh)], milestone=(k == KD - 1))
                    S.op("act", lambda e: e.activation(out=self.ch[:, jj, :], in_=self.ps[bh][:, 0:2], func=AF.Identity,
                                                       bias=self.col("bC", g), scale=1.0),
                         reads=[("ps", bh), "cpp"], writes=[("ch", jj)])

        def convX(gp):
            px = self.load_panel(win[:, base + 2 * c.DB + gp * 512:base + 2 * c.DB + (gp + 1) * 512], KD, 512)
            for jj in range(4):
                g = 4 * gp + jj
                if halo:
                    bh = nextbank()
                    for k in range(KD):
                        S.op("pe", lambda e: e.matmul(self.ps[bh][:, 0:2], px[0][:, k, jj * 128:(jj + 1) * 128], self.act_h[:, k, :],
                                                      start=(k == 0), stop=(k == KD - 1)),
                             reads=[px[1], "act_h"], writes=[("ps", bh)], milestone=(k == KD - 1))
                    S.op("dve", lambda e: e.scalar_tensor_tensor(out=self.thx[:, jj % 2, :], in0=self.ps[bh][:, 0:2],
                                                                 scalar=self.col("bX", g), in1=self.ch[:, jj, :],
                                                                 op0=ALU.add, op1=ALU.mult),
                         reads=[("ps", bh), ("ch", jj), "cpp"], writes=[("thx", jj % 2)])
                    S.op("dve", lambda e: e.tensor_scalar(out=self.carry[:, g, :], in0=self.thx[:, jj % 2, :],
                                                          scalar1=self.col("hm"), scalar2=None, op0=ALU.mult),
                         reads=[("thx", jj % 2), "cpp"], writes=[("carry", g)])
                bank = nextbank()
                self.zchunk_mm(px[0], px[1], jj, bank)
                cv, ck = self.cT(jj)
                hv, hk = self.hc(jj % 2)
                S.op("dve", lambda e: e.tensor_copy(out=hv[:, 0:2], in_=self.carry[:, g, :]), reads=[("carry", g)], writes=[hk])
                S.op("dve", lambda e: e.scalar_tensor_tensor(out=hv[:, 2:T + 2], in0=self.ps[bank][:, 0:T], scalar=self.col("bX", g),
                                                             in1=cv, op0=ALU.add, op1=ALU.mult),
                     reads=[("ps", bank), ck, "cpp"], writes=[hk])
                S.op("dve", lambda e: e.tensor_copy(out=self.carry[:, g, :], in_=hv[:, T:T + 2]), reads=[hk], writes=[("carry", g)])
                ts = 4 + jj % 2
                acc = self.tf[ts]
                S.op("dve", lambda e: e.tensor_scalar(out=acc[:], in0=hv[:, 0:T], scalar1=self.col("cw", 0 * NG + g), scalar2=None,
                                                      op0=ALU.mult), reads=[hk, "cpp"], writes=[("tf", ts)])
                S.op("dve", lambda e: e.scalar_tensor_tensor(out=acc[:], in0=hv[:, 1:T + 1], scalar=self.col("cw", 1 * NG + g),
                                                             in1=acc[:], op0=ALU.mult, op1=ALU.add),
                     reads=[hk, "cpp", ("tf", ts)], writes=[("tf", ts)])
                S.op("dve", lambda e: e.scalar_tensor_tensor(out=cv, in0=hv[:, 2:T + 2], scalar=self.col("cw", 2 * NG + g),
                                                             in1=acc[:], op0=ALU.mult, op1=ALU.add),
                     reads=[hk, "cpp", ("tf", ts)], writes=[ck])

        def convB(gp):
            pb = self.load_panel(win[:, base + gp * 512:base + (gp + 1) * 512], KD, 512)
            for jj in range(4):
                g = 4 * gp + jj
                bank = nextbank()
                self.zchunk_mm(pb[0], pb[1], jj, bank)
                cv, ck = self.cT(jj)
                yv, yk = self.yT(NH + g)
                S.op("dve", lambda e: e.scalar_tensor_tensor(out=yv, in0=self.ps[bank][:, 0:T], scalar=self.col("bB", g), in1=cv,
                                                             op0=ALU.add, op1=ALU.mult),
                     reads=[("ps", bank), ck, "cpp"], writes=[yk])

        items = []
        for gp in range(PA):
            items += [lambda gp=gp: convC(gp), lambda gp=gp: convX(gp), lambda gp=gp: convB(gp)]

        def item():
            if items:
                items.pop(0)()

        assert TC == 4
        V(0); chain(0)
        V(1); chain(1)
        item()
        SGU(0)
        if FILL_V:
            self.fillers(FILL_V, nextbank())
        V(2); chain(2)
        SGU(1)
        if FILL_V:
            self.fillers(FILL_V, nextbank())
        V(3); chain(3)
        item()
        SGU(2)
        item()
        SGU(3)
        while items:
            item()

        wo = self.w_out.ap()
        self.stat_first = True
        pending = []
        for cg in range(c.CG):
            pv, pk = self.load_panel(wo[:, cg * 512:(cg + 1) * 512], KD, 512)
            for c4 in range(4):
                cc = 4 * cg + c4
                bank = nextbank()
                for k in range(KD):
                    yv, yk = self.yT(k)
                    S.op("pe", lambda e: e.matmul(self.ps[bank][:, 0:T], pv[:, k, c4 * 128:(c4 + 1) * 128], yv,
                                                  start=(k == 0), stop=(k == KD - 1)),
                         reads=[pk, yk], writes=[("ps", bank)], milestone=(k == KD - 1))
                while pending:
                    self.stat_mm(pending.pop(0), 4, 5)
                S.op("dve", lambda e: e.scalar_tensor_tensor(out=self.res[:, cc, :], in0=self.ps[bank][:, 0:T],
                                                             scalar=self.col("bout", cc), in1=self.res[:, cc, :],
                                                             op0=ALU.add, op1=ALU.add),
                     reads=[("ps", bank), ("res", cc), "cpp"], writes=[("res", cc)])
                self.stat_prep(cc)
                pending.append(cc)
        while pending:
            self.stat_mm(pending.pop(0), 4, 5)
        if FILL_LN:
            self.fillers(FILL_LN, 7)

    def res_load(self):
        c, S = self.cfg, self.S
        src = self.xT.ap()[:, self.cur_t0:self.cur_t0 + c.T].rearrange("(k p) n -> p k n", p=128)
        S.dma("sp", self.res[:], src, "xres", writes=[("res", k) for k in range(c.KD)])

    def build(self):
        c, S = self.cfg, self.S
        T, KD = c.T, c.KD
        self.deferred = []
        self.late_done = False
        self.stat_slot = {}
        self.setup()
        for ti in range(c.NT):
            t0 = ti * T
            self.cur_t0 = t0
            self.pid = 0
            self.conv_mod = (SCR_MOD, 0) if (ti == 0 and SCR_MOD) else None
            self.scr_new = []
            src = self.xT.ap()[:, t0:t0 + T].rearrange("(k p) n -> p k n", p=128)
            S.dma("pool", self.act[:], src, "xact", writes=[("act", k) for k in range(KD)])
            halo = (ti == 0 and self.use_halo)
            self.res_state = 0
            if not self.deferred:
                self.res_load()
                self.res_state = 1
                self.res_m = -4
            self.ffn("a", scale_res=True, halo=halo, fill=FILL_LN, first_tile=(ti == 0))
            self.layernorm(0, 1, "lnag", "lnab", False, t0)
            self.mixer(ti, halo=halo)
            self.layernorm(4, 5, "lnmg", "lnmb", False, t0)
            self.ffn("c", scale_res=False, kouter_first=True)
            self.layernorm(0, 1, "lncg", "lncb", True, t0)
            assert self.pid == self.NPAN, (self.pid, self.NPAN)
            self.scr_valid.update(self.scr_new)
        while self.deferred:
            self.deferred.pop(0)()
        S.wait_all("sp", [("res", cc) for cc in range(KD)])
        return self.nc


def host_inputs(cfg, inputs, core):
    c = cfg
    f = lambda a: np.ascontiguousarray(np.asarray(a, dtype=np.float32))
    x = np.asarray(inputs["x"], dtype=np.float32).reshape(c.BATCH * c.SEQ, c.D)
    r0 = core * c.NTOK
    m = {}
    m["xT"] = f(x[r0:r0 + c.NTOK].T)
    has_halo = (r0 % c.SEQ) != 0
    m["xh"] = f(x[r0 - 2:r0].T) if has_halo else np.zeros((c.D, 2), np.float32)
    m["wg_a"] = f(inputs["ffa_gate"][0]); m["wu_a"] = f(inputs["ffa_up"][0]); m["wd_a"] = f(inputs["ffa_down"][0])
    m["wg_c"] = f(inputs["ffc_gate"][0]); m["wu_c"] = f(inputs["ffc_up"][0]); m["wd_c"] = f(inputs["ffc_down"][0])
    m["w_in"] = f(inputs["w_in"][0]); m["w_out"] = f(inputs["w_out"][0])
    pp = lambda v: np.asarray(v, np.float32).reshape(-1, 128).T
    b_in = np.asarray(inputs["b_in"][0], np.float32)
    DA, DB = c.DA, c.DB
    cols = [pp(inputs["ln_a_g"][0]), pp(inputs["ln_a_b"][0]), pp(inputs["ln_m_g"][0]), pp(inputs["ln_m_b"][0]),
            pp(inputs["ln_c_g"][0]), pp(inputs["ln_c_b"][0]), pp(inputs["b_out"][0]),
            pp(b_in[0:DA]), pp(b_in[2 * DA:2 * DA + DB]), pp(b_in[2 * DA + DB:2 * DA + 2 * DB]), pp(b_in[2 * DA + 2 * DB:]),
            pp(np.asarray(inputs["conv_w"][0], np.float32)[0]), pp(np.asarray(inputs["conv_w"][0], np.float32)[1]),
            pp(np.asarray(inputs["conv_w"][0], np.float32)[2]),
            np.full((128, 1), 1.0 if has_halo else 0.0, np.float32)]
    m["cpp"] = f(np.concatenate(cols, axis=1))
    assert m["cpp"].shape == (128, c.NCC)
    m["rows"] = f(np.concatenate([b_in[DA:2 * DA], np.asarray(inputs["b_s"][0], np.float32).reshape(-1)])[None, :])
    m["gvbv"] = f(np.stack([np.asarray(inputs["ln_v_g"][0], np.float32), np.asarray(inputs["ln_v_b"][0], np.float32)]))
    ws = np.asarray(inputs["w_s"][0], np.float32)
    m["wsT"] = f(np.transpose(ws, (2, 0, 1)).reshape(128, c.NH * 128))
    return m


_CACHE = {}


def run(cfg, inputs, use_halo=True, trace=False):
    key = (cfg.D, cfg.F, cfg.SEQ, cfg.BATCH, cfg.NCORES, cfg.T, use_halo)
    if key not in _CACHE:
        _CACHE[key] = Prog(cfg, use_halo).build()
    nc = _CACHE[key]
    in_maps = [host_inputs(cfg, inputs, core) for core in range(cfg.NCORES)]
    res = run_bass_kernel_spmd(nc, in_maps, core_ids=list(range(cfg.NCORES)), **({"trace": True} if trace else {}))
    out = np.empty((cfg.BATCH * cfg.SEQ, cfg.D), np.float32)
    for core in range(cfg.NCORES):
        out[core * cfg.NTOK:(core + 1) * cfg.NTOK] = res.results[core]["outT"].T
    return out.reshape(cfg.BATCH, cfg.SEQ, cfg.D), res


def kernel(**inputs):
    cfg = Cfg()
    out, _ = run(cfg, inputs)
    return out
```

```python
import numpy as np
import concourse.bass as bass
import concourse.mybir as mybir
from concourse.bass_utils import run_bass_kernel_spmd

F32 = mybir.dt.float32
BF16 = mybir.dt.bfloat16
AF = mybir.ActivationFunctionType
ALU = mybir.AluOpType
AX = mybir.AxisListType

ALPHA = float(2.0 ** 0.25)
LN_EPS = 1e-5
GELU_C = 0.7978845608028654
FILL_LN = 36
FILL_V = 40
SCR_MOD = 3


class Cfg:
    def __init__(self, D=2048, F=5632, SEQ=8192, BATCH=2, NCORES=8, T=512):
        self.D, self.F, self.SEQ, self.BATCH, self.NCORES, self.T = D, F, SEQ, BATCH, NCORES, T
        self.DA = D // 2
        self.DB = D - self.DA
        self.NH = self.DA // 128
        self.NG = self.DB // 128
        self.DIN = 2 * self.DA + 3 * self.DB
        self.KD = D // 128
        self.KF = F // 128
        self.NTOK = BATCH * SEQ // NCORES
        self.NT = self.NTOK // T
        self.TC = T // 128
        self.KG = 4
        self.KPG = self.KF // 4
        self.CG = self.KD // 4
        self.FG = F // 512
        self.PA = self.DA // 512
        assert F % 512 == 0 and self.KF % 4 == 0 and self.KD % 4 == 0 and self.DA % 512 == 0
        assert self.KPG * 512 <= 8192 and self.KD * 512 <= 8192
        assert self.NTOK % T == 0 and SEQ % self.NTOK == 0 or self.NTOK % SEQ == 0
        off = {}
        c = 0
        for nm in ("lnag", "lnab", "lnmg", "lnmb", "lncg", "lncb", "bout"):
            off[nm] = c
            c += self.KD
        for nm, n in (("bu", self.NH), ("bB", self.NG), ("bC", self.NG), ("bX", self.NG), ("cw", 3 * self.NG), ("hm", 1)):
            off[nm] = c
            c += n
        self.off = off
        self.NCC = c


class Sched:
    def __init__(self, nc):
        self.nc = nc
        self.eng = {"pe": nc.tensor, "act": nc.scalar, "dve": nc.vector, "pool": nc.gpsimd, "sp": nc.sync}
        self.prog = {e: nc.alloc_semaphore(f"prog_{e}") for e in ("pe", "act", "dve", "pool")}
        self.cnt = {e: 0 for e in self.prog}
        self.waited = {e: {} for e in self.eng}
        self.lastw = {}
        self.readers = {}
        self.ivw = []
        self.ivr = {}
        self.dma_sems = {}
        self.nwaits = 0
        self.nops = 0

    @staticmethod
    def _is_iv(k):
        return isinstance(k, tuple) and len(k) == 4 and k[0] == "IV"

    def _deps(self, reads, writes):
        evs = []
        for k in reads:
            if self._is_iv(k):
                for ent in self.ivw:
                    if ent[0] == k[1] and ent[1] < k[3] and k[2] < ent[2]:
                        evs.append(ent[3])
            elif k in self.lastw:
                evs.append(self.lastw[k])
        for k in writes:
            if self._is_iv(k):
                for ent in self.ivw:
                    if ent[0] == k[1] and ent[1] < k[3] and k[2] < ent[2]:
                        evs.append(ent[3])
                for (nm, lo, hi, src), ev in self.ivr.items():
                    if nm == k[1] and lo < k[3] and k[2] < hi:
                        evs.append(ev)
            else:
                if k in self.lastw:
                    evs.append(self.lastw[k])
                evs.extend(self.readers.get(k, {}).values())
        return evs

    def _wait(self, e, evs):
        need = {}
        for (sem, val, src) in evs:
            if e == "pe" and src == "pe":
                continue
            key = id(sem)
            if self.waited[e].get(key, 0) >= val:
                continue
            if key not in need or need[key][1] < val:
                need[key] = (sem, val)
        for key, (sem, val) in need.items():
            self.eng[e].wait_ge(sem, val)
            self.waited[e][key] = val
            self.nwaits += 1

    def _record(self, ev, reads, writes):
        for k in writes:
            if self._is_iv(k):
                self.ivw = [ent for ent in self.ivw if not (ent[0] == k[1] and k[2] <= ent[1] and ent[2] <= k[3])]
                for rk in [rk for rk in self.ivr if rk[0] == k[1] and k[2] <= rk[1] and rk[2] <= k[3]]:
                    del self.ivr[rk]
                self.ivw.append([k[1], k[2], k[3], ev])
            else:
                self.lastw[k] = ev
                self.readers[k] = {}
        for k in reads:
            if k in writes:
                continue
            if self._is_iv(k):
                self.ivr[(k[1], k[2], k[3], ev[2] if ev[2] != "dma" else id(ev[0]))] = ev
            else:
                self.readers.setdefault(k, {})[ev[2] if ev[2] != "dma" else id(ev[0])] = ev

    def op(self, e, fn, reads=(), writes=(), milestone=True):
        self._wait(e, self._deps(reads, writes))
        ins = fn(self.eng[e])
        self.nops += 1
        if milestone:
            ins.then_inc(self.prog[e], 1)
            self.cnt[e] += 1
            ev = (self.prog[e], self.cnt[e], e)
        else:
            ev = (self.prog[e], self.cnt[e] + 1, e)
        self._record(ev, reads, writes)
        return ins

    def dma(self, e, out, in_, semname, reads=(), writes=()):
        self._wait(e, self._deps(reads, writes))
        if semname not in self.dma_sems:
            self.dma_sems[semname] = [self.nc.alloc_semaphore(f"dma_{semname}"), 0]
        ent = self.dma_sems[semname]
        ent[1] += 16
        self.eng[e].dma_start(out=out, in_=in_).then_inc(ent[0], 16)
        ev = (ent[0], ent[1], "dma")
        self._record(ev, reads, writes)
        return ev

    def wait_all(self, e, keys):
        evs = []
        for k in keys:
            if k in self.lastw:
                evs.append(self.lastw[k])
            evs.extend(self.readers.get(k, {}).values())
        self._wait(e, evs)


def IV(name, lo, hi):
    return ("IV", name, lo, hi)


class Prog:
    def __init__(self, cfg, use_halo=True):
        self.cfg = cfg
        self.use_halo = use_halo
        c = cfg
        nc = self.nc = bass.Bass("TRN2", target_bir_lowering=False)
        T, KD, KF, NH, NG = c.T, c.KD, c.KF, c.NH, c.NG
        dt = lambda n, s: nc.dram_tensor(n, s, F32, kind="ExternalInput")
        self.xT = dt("xT", [c.D, c.NTOK])
        self.xh = dt("xh", [c.D, 2])
        self.wg = {"a": dt("wg_a", [c.D, c.F]), "c": dt("wg_c", [c.D, c.F])}
        self.wu = {"a": dt("wu_a", [c.D, c.F]), "c": dt("wu_c", [c.D, c.F])}
        self.wd = {"a": dt("wd_a", [c.F, c.D]), "c": dt("wd_c", [c.F, c.D])}
        self.w_in = dt("w_in", [c.D, c.DIN])
        self.w_out = dt("w_out", [c.D, c.D])
        self.cpp_d = dt("cpp", [128, c.NCC])
        self.rows_d = dt("rows", [1, 2 * c.DA])
        self.gvbv_d = dt("gvbv", [2, c.DA])
        self.wsT_d = dt("wsT", [128, NH * 128])
        self.outT = nc.dram_tensor("outT", [c.D, c.NTOK], F32, kind="ExternalOutput")

        sb = nc.alloc_sbuf_tensor
        self.res = sb("res", [128, KD, T], F32)
        self.act = sb("act", [128, KD, T], BF16)
        self.OFF_Y = NH * T * 2
        self.OFF_X = self.OFF_Y + KD * T
        xsz = 8 * T + 2 * 2 * (T + 2)
        self.HSZ = max(KF * T, self.OFF_X + xsz)
        self.hreg = sb("hreg", [128, self.HSZ], BF16)
        self.NB = 4
        self.pan = [sb(f"pan{i}", [128, 8192], BF16) for i in range(self.NB)]
        self.tf = [sb(f"tf{i}", [128, T], F32) for i in range(6)]
        self.NRS = 3
        self.rbsq = sb("rbsq", [128, self.NRS, 2, T], BF16)
        self.cpp = sb("cppsb", [128, c.NCC], F32)
        self.cpa = sb("cpasb", [128, 4 * KD], F32)
        self.rstg = sb("rstg", [33, c.DA], F32)
        self.brow = sb("brow", [128, 2 * c.DA], BF16)
        self.ones_pad = sb("ones_pad", [128, 128], BF16)
        self.gvt = sb("gvt", [128, c.DA], F32)
        self.bvt = sb("bvt", [128, c.DA], F32)
        self.wT = sb("wTbf", [128, NH * 128], BF16)
        self.ones_bf = sb("ones_bf", [128, 128], BF16)
        self.zeros_bf = sb("zeros_bf", [128, 512], BF16)
        self.carry = sb("carry", [128, NG, 2], F32)
        self.st = sb("vstat", [128, 8, NH], F32)
        self.vtb = sb("vtb", [128, c.DA], F32)
        self.tqb = sb("tqb", [128, c.DA], F32)
        self.nbb = sb("nbb", [128, 2, c.DA], BF16)
        self.res_h = sb("res_h", [128, KD, 2], F32)
        self.act_h = sb("act_h", [128, KD, 2], BF16)
        self.hid_h = sb("hid_h", [128, KF, 2], BF16)
        self.sgh = sb("sgh", [128, 2, 2], F32)
        self.th = sb("th", [128, KD, 2], F32)
        self.rbsq_h = sb("rbsq_h", [128, 2, KD, 2], BF16)
        self.lnh = sb("lnh", [128, 4, 2], F32)
        self.ch = sb("ch", [128, 4, 2], F32)
        self.thx = sb("thx", [128, 2, 2], F32)
        self.ps = [nc.alloc_psum_tensor(f"ps{i}", [128, 512], F32) for i in range(8)]
        self.S = Sched(nc)
        self.pan_i = 0
        self.rs_i = 0
        self.NPAN = 2 * (2 * c.FG + c.CG * c.KG) + 5 * c.PA + c.CG
        self.use_scr = c.NT >= 2
        self.scr = nc.dram_tensor("wscr", [self.NPAN, 128, 8192], BF16) if self.use_scr else None
        self.scr_valid = set()

    def hid(self, m):
        T = self.cfg.T
        return self.hreg[:, m * T:(m + 1) * T], IV("h", m * T, (m + 1) * T)

    def uT(self, h, lo=0, hi=None):
        T = self.cfg.T
        hi = T if hi is None else hi
        v = self.hreg[:, 0:self.OFF_Y].bitcast(F32)
        return v[:, h * T + lo:h * T + hi], IV("h", 2 * (h * T + lo), 2 * (h * T + hi))

    def yT(self, cc, lo=0, hi=None):
        T = self.cfg.T
        hi = T if hi is None else hi
        o = self.OFF_Y + cc * T
        return self.hreg[:, o + lo:o + hi], IV("h", o + lo, o + hi)

    def xreg_f32(self, off_f32, n):
        o = self.OFF_X + 2 * off_f32
        return self.hreg[:, o:o + 2 * n].bitcast(F32), IV("h", o, o + 2 * n)

    def xreg_bf(self, off_bf, n):
        o = self.OFF_X + off_bf
        return self.hreg[:, o:o + n], IV("h", o, o + n)

    def cT(self, j):
        return self.xreg_f32(j * self.cfg.T, self.cfg.T)

    def hc(self, s):
        T = self.cfg.T
        return self.xreg_f32(4 * T + s * (T + 2), T + 2)

    def vt(self):
        return self.vtb[:], "vt"

    def tq(self):
        return self.tqb[:], "tq"

    def nbf(self, s):
        return self.nbb[:, s, :], ("nb", s)

    def col(self, name, i=0):
        o = self.cfg.off[name] + i
        return self.cpp[:, o:o + 1]

    def load_panel(self, dram_ap, kk, ncols, noscr=False):
        s = self.pan_i % self.NB
        self.pan_i += 1
        if noscr:
            pid = -1
        else:
            pid = self.pid
            self.pid += 1
        n = kk * ncols
        flat = self.pan[s][:, 0:n]
        view = flat.rearrange("p (k n) -> p k n", k=kk)
        if self.scr is not None and pid in self.scr_valid:
            self.S.dma("pool", flat, self.scr.ap()[pid][:, 0:n], f"pan{s}", reads=[("scr", pid)], writes=[("pan", s)])
        else:
            src = dram_ap.rearrange("(k p) n -> p k n", p=128)
            self.S.dma("pool", view, src, f"pan{s}", writes=[("pan", s)])
            if self.scr is not None and pid >= 0 and self.conv_mod is not None and pid % self.conv_mod[0] == self.conv_mod[1]:
                self.S.dma("sp", self.scr.ap()[pid][:, 0:n], flat, f"st{s}", reads=[("pan", s)], writes=[("scr", pid)])
                self.scr_new.append(pid)
        return view, ("pan", s)

    def setup(self):
        c, S, nc = self.cfg, self.S, self.nc
        S.dma("sp", self.cpp[:], self.cpp_d.ap(), "c_cpp", writes=["cpp"])
        S.op("pool", lambda e: e.memset(self.ones_bf[:], 1.0 / c.D), writes=["ones_bf"])
        S.op("pool", lambda e: e.memset(self.zeros_bf[:], 0.0), writes=["zeros_bf"])
        S.op("pool", lambda e: e.memset(self.carry[:], 0.0), writes=["carry"])

    def setup_late(self):
        c, S, nc = self.cfg, self.S, self.nc
        NH = c.NH
        S.op("pool", lambda e: e.memset(self.brow[:], 0.0), writes=["rows"])
        S.op("pool", lambda e: e.memset(self.ones_pad[:], 0.0), writes=["ones_pad"])
        S.op("pool", lambda e: e.memset(self.ones_pad[0:1, :], 1.0), reads=["ones_pad"], writes=["ones_pad"])
        S.op("pool", lambda e: e.memset(self.ones_pad[32:33, :], 1.0), reads=["ones_pad"], writes=["ones_pad"])
        for hh in range(2):
            seg = slice(hh * c.DA, (hh + 1) * c.DA)
            S.dma("sp", self.rstg[0:1, :], self.rows_d.ap()[0:1, seg], "c_rows0", writes=["rstg0"])
            S.dma("sp", self.rstg[32:33, :], self.rows_d.ap()[0:1, seg], "c_rows1", writes=["rstg1"])
            S.op("dve", lambda e: e.tensor_copy(out=self.brow[0:1, seg], in_=self.rstg[0:1, :]), reads=["rstg0", "rows"], writes=["rows"])
            S.op("dve", lambda e: e.tensor_copy(out=self.brow[32:33, seg], in_=self.rstg[32:33, :]), reads=["rstg1", "rows"], writes=["rows"])
            S.op("dve", lambda e: e.tensor_tensor(out=self.rstg[32:33, :], in0=self.rstg[32:33, :], in1=self.brow[32:33, seg],
                                                  op=ALU.subtract), reads=["rstg1", "rows"], writes=["rstg1"])
            S.op("dve", lambda e: e.tensor_copy(out=self.brow[32:33, seg], in_=self.rstg[32:33, :]), reads=["rstg1", "rows"], writes=["rows"])
        S.dma("sp", self.gvt[:], self.gvbv_d.ap()[0:1, :].partition_broadcast(128), "c_gv", writes=["gvt"])
        S.dma("sp", self.bvt[:], self.gvbv_d.ap()[1:2, :].partition_broadcast(128), "c_bv", writes=["bvt"])
        wst = self.vtb[:]
        vkeys = [("vt", p) for p in range(c.PA)]
        S.dma("sp", wst, self.wsT_d.ap(), "c_ws", writes=vkeys)
        for h in range(NH):
            S.op("pool", lambda e: e.affine_select(out=wst[:, h * 128:(h + 1) * 128], in_=wst[:, h * 128:(h + 1) * 128],
                                                   pattern=[[1, 128]], compare_op=ALU.is_ge, fill=0.0, base=0,
                                                   channel_multiplier=-1), reads=vkeys, writes=vkeys)
        S.op("pool", lambda e: e.tensor_copy(out=self.wT[:], in_=wst), reads=vkeys, writes=["wT"])
        o = c.off
        S.op("dve", lambda e: e.tensor_scalar(out=self.cpa[:, 0:4 * c.KD], in0=self.cpp[:, o["lnag"]:o["lnag"] + 4 * c.KD],
                                              scalar1=ALPHA, scalar2=None, op0=ALU.mult), reads=["cpp"], writes=["cpa"])

    def gateup_evac(self, m, halo_ps=None):
        S, T = self.S, self.cfg.T
        bg, bu = m % 2, 2 + m % 2
        sg = self.tf[m % 2]
        S.op("act", lambda e: e.activation(out=sg[:], in_=self.ps[bg][:, 0:T], func=AF.Silu),
             reads=[("ps", bg)], writes=[("tf", m % 2)])
        hv, hk = self.hid(m)
        S.op("dve", lambda e: e.tensor_tensor(out=hv, in0=sg[:], in1=self.ps[bu][:, 0:T], op=ALU.mult),
             reads=[("tf", m % 2), ("ps", bu)], writes=[hk])

    def halo_load(self):
        S = self.S
        hsrc = self.xh.ap().rearrange("(k p) n -> p k n", p=128)
        S.dma("pool", self.act_h[:], hsrc, "xhact", writes=["act_h"])
        S.dma("sp", self.res_h[:], hsrc, "xhres", writes=["res_h"])

    def halo_gateup(self, m, pg, kg_, pu, ku_, cs):
        S, KD = self.S, self.cfg.KD
        bgh, buh = 4 + m % 2, 6 + m % 2
        for k in range(KD):
            S.op("pe", lambda e: e.matmul(self.ps[bgh][:, 0:2], pg[:, k, cs], self.act_h[:, k, :],
                                          start=(k == 0), stop=(k == KD - 1)),
                 reads=[kg_, "act_h"], writes=[("ps", bgh)], milestone=(k == KD - 1))
        for k in range(KD):
            S.op("pe", lambda e: e.matmul(self.ps[buh][:, 0:2], pu[:, k, cs], self.act_h[:, k, :],
                                          start=(k == 0), stop=(k == KD - 1)),
                 reads=[ku_, "act_h"], writes=[("ps", buh)], milestone=(k == KD - 1))
        S.op("act", lambda e: e.activation(out=self.sgh[:, m % 2, :], in_=self.ps[bgh][:, 0:2], func=AF.Silu),
             reads=[("ps", bgh)], writes=[("sgh", m % 2)])
        S.op("dve", lambda e: e.tensor_tensor(out=self.hid_h[:, m, :], in0=self.sgh[:, m % 2, :],
                                              in1=self.ps[buh][:, 0:2], op=ALU.mult),
             reads=[("sgh", m % 2), ("ps", buh)], writes=[("hid_h", m)])

    def after_mchunk(self, m):
        if self.deferred:
            self.deferred.pop(0)()
        elif self.res_state == 0:
            self.res_load()
            self.res_state = 1
            self.res_m = m

    def ffn(self, which, scale_res, halo=False, kouter_first=False, fill=0, first_tile=False):
        c, S = self.cfg, self.S
        T, KD = c.T, c.KD
        wg, wu, wd = self.wg[which].ap(), self.wu[which].ap(), self.wd[which].ap()
        for j in range(c.FG):
            if j == 0 and first_tile:
                for mm in range(4):
                    pgs, kgs = self.load_panel(wg[:, mm * 128:(mm + 1) * 128], KD, 128, noscr=True)
                    pus, kus = self.load_panel(wu[:, mm * 128:(mm + 1) * 128], KD, 128, noscr=True)
                    if mm == 0:
                        self.pid += 2
                    m = mm
                    bg, bu = m % 2, 2 + m % 2
                    for k in range(KD):
                        S.op("pe", lambda e: e.matmul(self.ps[bg][:, 0:T], pgs[:, k, :], self.act[:, k, :],
                                                      start=(k == 0), stop=(k == KD - 1)),
                             reads=[kgs, ("act", k)], writes=[("ps", bg)], milestone=(k == KD - 1))
                    for k in range(KD):
                        S.op("pe", lambda e: e.matmul(self.ps[bu][:, 0:T], pus[:, k, :], self.act[:, k, :],
                                                      start=(k == 0), stop=(k == KD - 1)),
                             reads=[kus, ("act", k)], writes=[("ps", bu)], milestone=(k == KD - 1))
                    self.gateup_evac(m)
                    if halo:
                        if mm == 0:
                            self.halo_load()
                        self.halo_gateup(m, pgs, kgs, pus, kus, slice(0, 128))
                    self.after_mchunk(m)
                if not self.late_done:
                    self.setup_late()
                    self.late_done = True
                if c.FG > 1:
                    continue
                pg = pu = None
                mm_list = []
            else:
                pg, kg_ = self.load_panel(wg[:, j * 512:(j + 1) * 512], KD, 512)
                pu, ku_ = self.load_panel(wu[:, j * 512:(j + 1) * 512], KD, 512)
                mm_list = list(range(4))
            if kouter_first and j == 0:
                for k in range(KD):
                    for mm in (0, 1):
                        for (pan, pk_, bank) in ((pg, kg_, mm % 2), (pu, ku_, 2 + mm % 2)):
                            S.op("pe", lambda e: e.matmul(self.ps[bank][:, 0:T], pan[:, k, mm * 128:(mm + 1) * 128], self.act[:, k, :],
                                                          start=(k == 0), stop=(k == KD - 1)),
                                 reads=[pk_, ("act", k)], writes=[("ps", bank)], milestone=(k == KD - 1))
                for mm in (0, 1):
                    self.gateup_evac(mm)
                mm_list = [2, 3]
            for mm in mm_list:
                m = 4 * j + mm
                bg, bu = m % 2, 2 + m % 2
                for k in range(KD):
                    S.op("pe", lambda e: e.matmul(self.ps[bg][:, 0:T], pg[:, k, mm * 128:(mm + 1) * 128], self.act[:, k, :],
                                                  start=(k == 0), stop=(k == KD - 1)),
                         reads=[kg_, ("act", k)], writes=[("ps", bg)], milestone=(k == KD - 1))
                for k in range(KD):
                    S.op("pe", lambda e: e.matmul(self.ps[bu][:, 0:T], pu[:, k, mm * 128:(mm + 1) * 128], self.act[:, k, :],
                                                  start=(k == 0), stop=(k == KD - 1)),
                         reads=[ku_, ("act", k)], writes=[("ps", bu)], milestone=(k == KD - 1))
                self.gateup_evac(m)
                if halo:
                    self.halo_gateup(m, pg, kg_, pu, ku_, slice(mm * 128, (mm + 1) * 128))
                self.after_mchunk(m)
            if j == 0 and not self.late_done:
                self.setup_late()
                self.late_done = True
            if j == c.FG - 1:
                while self.deferred:
                    self.deferred.pop(0)()
                if self.res_state == 0:
                    self.res_load()
                    self.res_state = 1
                    self.res_m = -100
            if scale_res and self.res_state == 1 and j == c.FG - 1:
                self.res_state = 2
                for cc in range(KD):
                    S.op("dve", lambda e: e.tensor_scalar(out=self.res[:, cc, :], in0=self.res[:, cc, :], scalar1=ALPHA,
                                                          scalar2=None, op0=ALU.mult),
                         reads=[("res", cc)], writes=[("res", cc)])
                if halo:
                    S.op("dve", lambda e: e.tensor_scalar(out=self.res_h[:], in0=self.res_h[:], scalar1=ALPHA, scalar2=None,
                                                          op0=ALU.mult), reads=["res_h"], writes=["res_h"])
        assert not self.deferred and (not scale_res or self.res_state == 2)
        self.stat_first = True
        self.hblk = 0
        pending = []

        def evac(cc, bank):
            S.op("dve", lambda e: e.scalar_tensor_tensor(out=self.res[:, cc, :], in0=self.ps[bank][:, 0:T], scalar=0.5,
                                                         in1=self.res[:, cc, :], op0=ALU.mult, op1=ALU.add),
                 reads=[("ps", bank), ("res", cc)], writes=[("res", cc)])
            self.stat_prep(cc)
            pending.append(cc)

        for cg in range(c.CG):
            for kg in range(c.KG):
                pd, kd_ = self.load_panel(wd[kg * c.KPG * 128:(kg + 1) * c.KPG * 128, cg * 512:(cg + 1) * 512], c.KPG, 512)
                for c4 in range(4):
                    cc = 4 * cg + c4
                    bank = 4 + c4
                    for kk in range(c.KPG):
                        k = kg * c.KPG + kk
                        hv, hk = self.hid(k)
                        S.op("pe", lambda e: e.matmul(self.ps[bank][:, 0:T], pd[:, kk, c4 * 128:(c4 + 1) * 128], hv,
                                                      start=(kg == 0 and kk == 0), stop=(kg == c.KG - 1 and kk == c.KPG - 1)),
                             reads=[kd_, hk], writes=[("ps", bank)], milestone=(kk == c.KPG - 1))
                    if halo:
                        bh = 2 + self.hblk % 2
                        self.hblk += 1
                        for kk in range(c.KPG):
                            k = kg * c.KPG + kk
                            S.op("pe", lambda e: e.matmul(self.ps[bh][:, 0:2], pd[:, kk, c4 * 128:(c4 + 1) * 128], self.hid_h[:, k, :],
                                                          start=(kk == 0), stop=(kk == c.KPG - 1)),
                                 reads=[kd_, ("hid_h", k)], writes=[("ps", bh)], milestone=(kk == c.KPG - 1))
                        S.op("dve", lambda e: e.scalar_tensor_tensor(out=self.res_h[:, cc, :], in0=self.ps[bh][:, 0:2], scalar=0.5,
                                                                     in1=self.res_h[:, cc, :], op0=ALU.mult, op1=ALU.add),
                             reads=[("ps", bh), "res_h"], writes=["res_h"])
                    while pending:
                        self.stat_mm(pending.pop(0), 0, 1)
                    if kg == c.KG - 1:
                        evac(cc, bank)
        while pending:
            self.stat_mm(pending.pop(0), 0, 1)
        if halo:
            self.halo_ln()
        if fill:
            self.fillers(fill, 3)

    def halo_ln(self):
        c, S = self.cfg, self.S
        KD = c.KD
        S.op("act", lambda e: e.activation(out=self.rbsq_h[:, 0, :, :], in_=self.res_h[:], func=AF.Copy),
             reads=["res_h"], writes=["rbh"])
        S.op("act", lambda e: e.activation(out=self.rbsq_h[:, 1, :, :], in_=self.res_h[:], func=AF.Square),
             reads=["res_h"], writes=["sqh"])
        for cc in range(KD):
            S.op("pe", lambda e: e.matmul(self.ps[2][:, 0:2], self.ones_bf[:], self.rbsq_h[:, 0, cc, :], start=(cc == 0),
                                          stop=(cc == KD - 1)), reads=["rbh", "ones_bf"], writes=[("ps", 2)], milestone=(cc == KD - 1))
        for cc in range(KD):
            S.op("pe", lambda e: e.matmul(self.ps[3][:, 0:2], self.ones_bf[:], self.rbsq_h[:, 1, cc, :], start=(cc == 0),
                                          stop=(cc == KD - 1)), reads=["sqh", "ones_bf"], writes=[("ps", 3)], milestone=(cc == KD - 1))
        mean, var, rstd = self.lnh[:, 0, :], self.lnh[:, 1, :], self.lnh[:, 2, :]
        S.op("dve", lambda e: e.tensor_copy(out=mean, in_=self.ps[2][:, 0:2]), reads=[("ps", 2)], writes=["lnh"])
        S.op("dve", lambda e: e.tensor_tensor(out=var, in0=mean, in1=mean, op=ALU.mult), reads=["lnh"], writes=["lnh"])
        S.op("dve", lambda e: e.scalar_tensor_tensor(out=var, in0=self.ps[3][:, 0:2], scalar=LN_EPS, in1=var, op0=ALU.add,
                                                     op1=ALU.subtract), reads=[("ps", 3), "lnh"], writes=["lnh"])
        S.op("act", lambda e: e.activation(out=var, in_=var, func=AF.Sqrt), reads=["lnh"], writes=["lnh"])
        S.op("dve", lambda e: e.reciprocal(out=rstd, in_=var), reads=["lnh"], writes=["lnh"])
        mb = mean.unsqueeze(1).to_broadcast([128, KD, 2])
        rb = rstd.unsqueeze(1).to_broadcast([128, KD, 2])
        go, bo = c.off["lnag"], c.off["lnab"]
        gb = self.cpp[:, go:go + KD].unsqueeze(2).to_broadcast([128, KD, 2])
        bb = self.cpp[:, bo:bo + KD].unsqueeze(2).to_broadcast([128, KD, 2])
        S.op("dve", lambda e: e.tensor_tensor(out=self.th[:], in0=self.res_h[:], in1=mb, op=ALU.subtract),
             reads=["res_h", "lnh"], writes=["th"])
        S.op("dve", lambda e: e.tensor_tensor(out=self.th[:], in0=self.th[:], in1=rb, op=ALU.mult), reads=["th", "lnh"], writes=["th"])
        S.op("dve", lambda e: e.tensor_tensor(out=self.th[:], in0=self.th[:], in1=gb, op=ALU.mult), reads=["th", "cpp"], writes=["th"])
        S.op("dve", lambda e: e.tensor_tensor(out=self.act_h[:], in0=self.th[:], in1=bb, op=ALU.add), reads=["th", "cpp"],
             writes=["act_h"])

    def stat_prep(self, cc):
        S, T = self.S, self.cfg.T
        s = self.rs_i % self.NRS
        self.rs_i += 1
        S.op("act", lambda e: e.activation(out=self.rbsq[:, s, 0, :], in_=self.res[:, cc, :], func=AF.Copy),
             reads=[("res", cc)], writes=[("rb", s)])
        S.op("act", lambda e: e.activation(out=self.rbsq[:, s, 1, :], in_=self.res[:, cc, :], func=AF.Square),
             reads=[("res", cc)], writes=[("sq", s)])
        self.stat_slot[cc] = s

    def stat_mm(self, cc, b1, b2):
        S, T, KD = self.S, self.cfg.T, self.cfg.KD
        s = self.stat_slot[cc]
        first = self.stat_first
        self.stat_n = 1 if first else self.stat_n + 1
        self.stat_first = False
        last = (self.stat_n == KD)
        S.op("pe", lambda e: e.matmul(self.ps[b1][:, 0:T], self.ones_bf[:], self.rbsq[:, s, 0, :], start=first, stop=last),
             reads=[("rb", s), "ones_bf"], writes=[("ps", b1)], milestone=True)
        S.op("pe", lambda e: e.matmul(self.ps[b2][:, 0:T], self.ones_bf[:], self.rbsq[:, s, 1, :], start=first, stop=last),
             reads=[("sq", s), "ones_bf"], writes=[("ps", b2)], milestone=True)

    def fillers(self, n, bank):
        S = self.S
        for i in range(n):
            S.op("pe", lambda e: e.matmul(self.ps[bank][:, 0:512], self.zeros_bf[:, 0:128], self.zeros_bf[:, :], start=True, stop=True),
                 reads=["zeros_bf"], writes=[("ps", bank)], milestone=(i == n - 1))

    def layernorm(self, b1, b2, gname, bname, final, t0):
        c, S = self.cfg, self.S
        T, KD = c.T, c.KD
        mean, var, rstd = self.tf[2], self.tf[3], self.tf[4]
        S.op("dve", lambda e: e.tensor_copy(out=mean[:], in_=self.ps[b1][:, 0:T]), reads=[("ps", b1)], writes=[("tf", 2)])
        S.op("dve", lambda e: e.tensor_tensor(out=var[:], in0=mean[:], in1=mean[:], op=ALU.mult),
             reads=[("tf", 2)], writes=[("tf", 3)])
        S.op("dve", lambda e: e.scalar_tensor_tensor(out=var[:], in0=self.ps[b2][:, 0:T], scalar=LN_EPS, in1=var[:],
                                                     op0=ALU.add, op1=ALU.subtract),
             reads=[("ps", b2), ("tf", 3)], writes=[("tf", 3)])
        S.op("act", lambda e: e.activation(out=var[:], in_=var[:], func=AF.Sqrt), reads=[("tf", 3)], writes=[("tf", 3)])
        S.op("dve", lambda e: e.reciprocal(out=rstd[:], in_=var[:]), reads=[("tf", 3)], writes=[("tf", 4)])
        go, bo = c.off[gname], c.off[bname]
        ao = {"lnag": 0, "lnab": KD, "lnmg": 2 * KD, "lnmb": 3 * KD}
        tslots = (5, 3) if final else (0, 1)

        def apply(cc):
            ts = tslots[cc % 2]
            t = self.tf[ts]
            S.op("dve", lambda e: e.tensor_tensor(out=t[:], in0=self.res[:, cc, :], in1=mean[:], op=ALU.subtract),
                 reads=[("res", cc), ("tf", 2)], writes=[("tf", ts)])
            S.op("dve", lambda e: e.tensor_tensor(out=t[:], in0=t[:], in1=rstd[:], op=ALU.mult),
                 reads=[("tf", ts), ("tf", 4)], writes=[("tf", ts)])
            if final:
                S.op("act", lambda e: e.activation(out=self.res[:, cc, :], in_=t[:], func=AF.Identity,
                                                   bias=self.cpp[:, bo + cc:bo + cc + 1], scale=self.cpp[:, go + cc:go + cc + 1]),
                     reads=[("tf", ts), "cpp"], writes=[("res", cc)])
            else:
                ag, ab = ao[gname] + cc, ao[bname] + cc
                S.op("act", lambda e: e.activation(out=self.act[:, cc, :], in_=t[:], func=AF.Identity,
                                                   bias=self.cpp[:, bo + cc:bo + cc + 1], scale=self.cpp[:, go + cc:go + cc + 1]),
                     reads=[("tf", ts), "cpp"], writes=[("act", cc)])
                S.op("act", lambda e: e.activation(out=self.res[:, cc, :], in_=t[:], func=AF.Identity,
                                                   bias=self.cpa[:, ab:ab + 1], scale=self.cpa[:, ag:ag + 1]),
                     reads=[("tf", ts), "cpa"], writes=[("res", cc)])

        def store():
            dst = self.outT.ap()[:, t0:t0 + T].rearrange("(k p) n -> p k n", p=128)
            S.dma("sp", dst, self.res[:], "out", reads=[("res", cc) for cc in range(KD)])

        if final:
            for cc in range(KD):
                self.deferred.append(lambda cc=cc: apply(cc))
            self.deferred.append(store)
        else:
            for cc in range(KD):
                apply(cc)

    def zchunk_mm(self, pv, pk, jj, bank):
        S, T, KD = self.S, self.cfg.T, self.cfg.KD
        for k in range(KD):
            S.op("pe", lambda e: e.matmul(self.ps[bank][:, 0:T], pv[:, k, jj * 128:(jj + 1) * 128], self.act[:, k, :],
                                          start=(k == 0), stop=(k == KD - 1)),
                 reads=[pk, ("act", k)], writes=[("ps", bank)], milestone=(k == KD - 1))

    def mixer(self, tile_idx, halo=False):
        c, S = self.cfg, self.S
        T, KD, NH, NG, DA, PA, TC = c.T, c.KD, c.NH, c.NG, c.DA, c.PA, c.TC
        win = self.w_in.ap()
        self.zb = 0

        def nextbank():
            b = self.zb % 4
            self.zb += 1
            return b

        def u_evac(h, bank):
            xs, qs = h % 2, 2 + h % 2
            tx, tq_ = self.tf[xs], self.tf[qs]
            S.op("act", lambda e: e.activation(out=tx[:], in_=self.ps[bank][:, 0:T], func=AF.Identity,
                                               bias=self.col("bu", h), scale=1.0),
                 reads=[("ps", bank), "cpp"], writes=[("tf", xs)])
            S.op("act", lambda e: e.activation(out=tq_[:], in_=tx[:], func=AF.Square), reads=[("tf", xs)], writes=[("tf", qs)])
            S.op("dve", lambda e: e.tensor_scalar(out=tq_[:], in0=tq_[:], scalar1=0.044715, scalar2=1.0, op0=ALU.mult,
                                                  op1=ALU.add), reads=[("tf", qs)], writes=[("tf", qs)])
            S.op("dve", lambda e: e.tensor_tensor(out=tq_[:], in0=tq_[:], in1=tx[:], op=ALU.mult),
                 reads=[("tf", qs), ("tf", xs)], writes=[("tf", qs)])
            S.op("act", lambda e: e.activation(out=tq_[:], in_=tq_[:], func=AF.Sigmoid, scale=2.0 * GELU_C),
                 reads=[("tf", qs)], writes=[("tf", qs)])
            uv, uk = self.uT(h)
            S.op("dve", lambda e: e.tensor_tensor(out=uv, in0=tq_[:], in1=tx[:], op=ALU.mult),
                 reads=[("tf", qs), ("tf", xs)], writes=[uk])

        for p in range(PA):
            pv, pk = self.load_panel(win[:, p * 512:(p + 1) * 512], KD, 512)
            if p == 0:
                banks = [nextbank() for _ in range(4)]
                for k in range(KD):
                    for jj in range(4):
                        S.op("pe", lambda e: e.matmul(self.ps[banks[jj]][:, 0:T], pv[:, k, jj * 128:(jj + 1) * 128], self.act[:, k, :],
                                                      start=(k == 0), stop=(k == KD - 1)),
                             reads=[pk, ("act", k)], writes=[("ps", banks[jj])], milestone=(k == KD - 1))
                for jj in range(4):
                    u_evac(4 * p + jj, banks[jj])
            else:
                for jj in range(4):
                    bank = nextbank()
                    self.zchunk_mm(pv, pk, jj, bank)
                    u_evac(4 * p + jj, bank)

        vp = [self.load_panel(win[:, DA + p * 512:DA + (p + 1) * 512], KD, 512) for p in range(PA)]
        vt, vtk = self.vt()
        tqv, tqk = self.tq()
        st = self.st

        def vbanks(tc):
            return [4 + 2 * (tc % 2) + p for p in range(PA)]

        def V(tc):
            for p in range(PA):
                pv, pk = vp[p]
                bank = vbanks(tc)[p]
                for k in range(KD):
                    S.op("pe", lambda e: e.matmul(self.ps[bank][:, 0:512], self.act[:, k, tc * 128:(tc + 1) * 128], pv[:, k, :],
                                                  start=(k == 0), stop=False),
                         reads=[pk, ("act", k)], writes=[("ps", bank)], milestone=False)
                S.op("pe", lambda e: e.matmul(self.ps[bank][:, 0:512], self.ones_pad[:, :],
                                              self.brow[:, p * 512:(p + 1) * 512], start=False, stop=True),
                     reads=["ones_pad", "rows"], writes=[("ps", bank)], milestone=True)

        def chain(tc):
            banks = vbanks(tc)
            nb, nbk = self.nbf(tc % 2)
            HP = 4
            sls = [slice(p * 512, (p + 1) * 512) for p in range(PA)]
            tk = [("tq", p) for p in range(PA)]
            vk = [("vt", p) for p in range(PA)]
            for p in range(PA):
                S.op("act", lambda e: e.activation(out=tqv[:, sls[p]], in_=self.ps[banks[p]][:, 0:512], func=AF.Square),
                     reads=[("ps", banks[p])], writes=[tk[p]])
            for p in range(PA):
                S.op("dve", lambda e: e.tensor_scalar(out=tqv[:, sls[p]], in0=tqv[:, sls[p]], scalar1=0.044715, scalar2=1.0,
                                                      op0=ALU.mult, op1=ALU.add), reads=[tk[p]], writes=[tk[p]])
                S.op("dve", lambda e: e.tensor_tensor(out=tqv[:, sls[p]], in0=tqv[:, sls[p]], in1=self.ps[banks[p]][:, 0:512],
                                                      op=ALU.mult), reads=[tk[p], ("ps", banks[p])], writes=[tk[p]])
            for p in range(PA):
                S.op("act", lambda e: e.activation(out=tqv[:, sls[p]], in_=tqv[:, sls[p]], func=AF.Sigmoid, scale=2.0 * GELU_C),
                     reads=[tk[p]], writes=[tk[p]])
            for p in range(PA):
                S.op("dve", lambda e: e.tensor_tensor(out=vt[:, sls[p]], in0=tqv[:, sls[p]], in1=self.ps[banks[p]][:, 0:512],
                                                      op=ALU.mult), reads=[tk[p], ("ps", banks[p])], writes=[vk[p]])
            v3 = vt.rearrange("p (h d) -> p h d", h=NH)
            q3 = tqv.rearrange("p (h d) -> p h d", h=NH)
            for p in range(PA):
                hs = slice(p * HP, (p + 1) * HP)
                S.op("dve", lambda e: e.tensor_reduce(out=st[:, 0, hs], in_=v3[:, hs, :], axis=AX.X, op=ALU.add),
                     reads=[vk[p]], writes=[("st1", p)])
                S.op("act", lambda e: e.activation(out=tqv[:, sls[p]], in_=vt[:, sls[p]], func=AF.Square),
                     reads=[vk[p]], writes=[tk[p]])
            for p in range(PA):
                hs = slice(p * HP, (p + 1) * HP)
                S.op("dve", lambda e: e.tensor_reduce(out=st[:, 1, hs], in_=q3[:, hs, :], axis=AX.X, op=ALU.add),
                     reads=[tk[p]], writes=[("st2", p)])
            s1k = [("st1", p) for p in range(PA)]
            s2k = [("st2", p) for p in range(PA)]
            S.op("dve", lambda e: e.tensor_scalar(out=st[:, 2, :], in0=st[:, 0, :], scalar1=1.0 / 128, scalar2=None, op0=ALU.mult),
                 reads=s1k, writes=["st"])
            S.op("dve", lambda e: e.tensor_tensor(out=st[:, 3, :], in0=st[:, 2, :], in1=st[:, 2, :], op=ALU.mult),
                 reads=["st"], writes=["st"])
            S.op("dve", lambda e: e.tensor_scalar(out=st[:, 4, :], in0=st[:, 1, :], scalar1=1.0 / 128, scalar2=LN_EPS,
                                                  op0=ALU.mult, op1=ALU.add), reads=["st"] + s2k, writes=["st"])
            S.op("dve", lambda e: e.tensor_tensor(out=st[:, 4, :], in0=st[:, 4, :], in1=st[:, 3, :], op=ALU.subtract),
                 reads=["st"], writes=["st"])
            S.op("act", lambda e: e.activation(out=st[:, 5, :], in_=st[:, 4, :], func=AF.Sqrt), reads=["st"], writes=["st"])
            S.op("dve", lambda e: e.reciprocal(out=st[:, 6, :], in_=st[:, 5, :]), reads=["st"], writes=["st"])
            mb = st[:, 2, :].unsqueeze(2).to_broadcast([128, NH, 128])
            rb = st[:, 6, :].unsqueeze(2).to_broadcast([128, NH, 128])
            S.op("dve", lambda e: e.tensor_tensor(out=v3, in0=v3, in1=mb, op=ALU.subtract), reads=vk + ["st"], writes=vk)
            S.op("dve", lambda e: e.tensor_tensor(out=v3, in0=v3, in1=rb, op=ALU.mult), reads=vk + ["st"], writes=vk)
            S.op("dve", lambda e: e.tensor_tensor(out=vt, in0=vt, in1=self.gvt[:], op=ALU.mult), reads=vk + ["gvt"], writes=vk)
            S.op("dve", lambda e: e.tensor_tensor(out=nb, in0=vt, in1=self.bvt[:], op=ALU.add), reads=vk + ["bvt"], writes=[nbk])
            for p in range(PA):
                self.S.lastw[("st1", p)] = self.S.lastw["st"]
                self.S.lastw[("st2", p)] = self.S.lastw["st"]

        def SGU(tc):
            banks = vbanks(tc)
            nb, nbk = self.nbf(tc % 2)
            for p in range(PA):
                bank = banks[p]
                S.op("pe", lambda e: e.matmul(self.ps[bank][:, 0:512], self.ones_pad[:, :],
                                              self.brow[:, DA + p * 512:DA + (p + 1) * 512], start=True, stop=False),
                     reads=["ones_pad", "rows"], writes=[("ps", bank)], milestone=False)
                for hh in range(4):
                    h = 4 * p + hh
                    oc = hh * 128
                    S.op("pe", lambda e: e.matmul(self.ps[bank][:, oc:oc + 128], nb[:, h * 128:(h + 1) * 128],
                                                  self.wT[:, h * 128:(h + 1) * 128], start=False, stop=(hh == 3)),
                         reads=[nbk, "wT"], writes=[("ps", bank)], milestone=(hh == 3))
            for p in range(PA):
                bank = banks[p]
                for hh in range(4):
                    h = 4 * p + hh
                    uv, uk = self.uT(h, tc * 128, (tc + 1) * 128)
                    yv, yk = self.yT(h, tc * 128, (tc + 1) * 128)
                    S.op("dve", lambda e: e.tensor_tensor(out=yv, in0=self.ps[bank][:, hh * 128:(hh + 1) * 128], in1=uv, op=ALU.mult),
                         reads=[("ps", bank), uk], writes=[yk])

        base = 2 * DA

        def convC(gp):
            pc = self.load_panel(win[:, base + c.DB + gp * 512:base + c.DB + (gp + 1) * 512], KD, 512)
            for jj in range(4):
                g = 4 * gp + jj
                bank = nextbank()
                self.zchunk_mm(pc[0], pc[1], jj, bank)
                cv, ck = self.cT(jj)
                S.op("act", lambda e: e.activation(out=cv, in_=self.ps[bank][:, 0:T], func=AF.Identity, bias=self.col("bC", g),
                                                   scale=1.0), reads=[("ps", bank), "cpp"], writes=[ck])
                if halo:
                    bh = nextbank()
                    for k in range(KD):
                        S.op("pe", lambda e: e.matmul(self.ps[bh][:, 0:2], pc[0][:, k, jj * 128:(jj + 1) * 128], self.act_h[:, k, :],
                                                      start=(k == 0), stop=(k == KD - 1)),
                             reads=[pc[1], "act_h"], writes=[("ps", bh)], milestone=(k == KD - 1))
                    S.op("act", lambda e: e.activation(out=self.ch[:, jj, :], in_=self.ps[bh][:, 0:2], func=AF.Identity,
                                                       bias=self.col("bC", g), scale=1.0),
                         reads=[("ps", bh), "cpp"], writes=[("ch", jj)])

        def convX(gp):
            px = self.load_panel(win[:, base + 2 * c.DB + gp * 512:base + 2 * c.DB + (gp + 1) * 512], KD, 512)
            for jj in range(4):
                g = 4 * gp + jj
                if halo:
                    bh = nextbank()
                    for k in range(KD):
                        S.op("pe", lambda e: e.matmul(self.ps[bh][:, 0:2], px[0][:, k, jj * 128:(jj + 1) * 128], self.act_h[:, k, :],
                                                      start=(k == 0), stop=(k == KD - 1)),
                             reads=[px[1], "act_h"], writes=[("ps", bh)], milestone=(k == KD - 1))
                    S.op("dve", lambda e: e.scalar_tensor_tensor(out=self.thx[:, jj % 2, :], in0=self.ps[bh][:, 0:2],
                                                                 scalar=self.col("bX", g), in1=self.ch[:, jj, :],
                                                                 op0=ALU.add, op1=ALU.mult),
                         reads=[("ps", bh), ("ch", jj), "cpp"], writes=[("thx", jj % 2)])
                    S.op("dve", lambda e: e.tensor_scalar(out=self.carry[:, g, :], in0=self.thx[:, jj % 2, :],
                                                          scalar1=self.col("hm"), scalar2=None, op0=ALU.mult),
                         reads=[("thx", jj % 2), "cpp"], writes=[("carry", g)])
                bank = nextbank()
                self.zchunk_mm(px[0], px[1], jj, bank)
                cv, ck = self.cT(jj)
                hv, hk = self.hc(jj % 2)
                S.op("dve", lambda e: e.tensor_copy(out=hv[:, 0:2], in_=self.carry[:, g, :]), reads=[("carry", g)], writes=[hk])
                S.op("dve", lambda e: e.scalar_tensor_tensor(out=hv[:, 2:T + 2], in0=self.ps[bank][:, 0:T], scalar=self.col("bX", g),
                                                             in1=cv, op0=ALU.add, op1=ALU.mult),
                     reads=[("ps", bank), ck, "cpp"], writes=[hk])
                S.op("dve", lambda e: e.tensor_copy(out=self.carry[:, g, :], in_=hv[:, T:T + 2]), reads=[hk], writes=[("carry", g)])
                ts = 4 + jj % 2
                acc = self.tf[ts]
                S.op("dve", lambda e: e.tensor_scalar(out=acc[:], in0=hv[:, 0:T], scalar1=self.col("cw", 0 * NG + g), scalar2=None,
                                                      op0=ALU.mult), reads=[hk, "cpp"], writes=[("tf", ts)])
                S.op("dve", lambda e: e.scalar_tensor_tensor(out=acc[:], in0=hv[:, 1:T + 1], scalar=self.col("cw", 1 * NG + g),
                                                             in1=acc[:], op0=ALU.mult, op1=ALU.add),
                     reads=[hk, "cpp", ("tf", ts)], writes=[("tf", ts)])
                S.op("dve", lambda e: e.scalar_tensor_tensor(out=cv, in0=hv[:, 2:T + 2], scalar=self.col("cw", 2 * NG + g),
                                                             in1=acc[:], op0=ALU.mult, op1=ALU.add),
                     reads=[hk, "cpp", ("tf", ts)], writes=[ck])

        def convB(gp):
            pb = self.load_panel(win[:, base + gp * 512:base + (gp + 1) * 512], KD, 512)
            for jj in range(4):
                g = 4 * gp + jj
                bank = nextbank()
                self.zchunk_mm(pb[0], pb[1], jj, bank)
                cv, ck = self.cT(jj)
                yv, yk = self.yT(NH + g)
                S.op("dve", lambda e: e.scalar_tensor_tensor(out=yv, in0=self.ps[bank][:, 0:T], scalar=self.col("bB", g), in1=cv,
                                                             op0=ALU.add, op1=ALU.mult),
                     reads=[("ps", bank), ck, "cpp"], writes=[yk])

        items = []
        for gp in range(PA):
            items += [lambda gp=gp: convC(gp), lambda gp=gp: convX(gp), lambda gp=gp: convB(gp)]

        def item():
            if items:
                items.pop(0)()

        assert TC == 4
        V(0); chain(0)
        V(1); chain(1)
        item()
        SGU(0)
        if FILL_V:
            self.fillers(FILL_V, nextbank())
        V(2); chain(2)
        SGU(1)
        if FILL_V:
            self.fillers(FILL_V, nextbank())
        V(3); chain(3)
        item()
        SGU(2)
        item()
        SGU(3)
        while items:
            item()

        wo = self.w_out.ap()
        self.stat_first = True
        pending = []
        for cg in range(c.CG):
            pv, pk = self.load_panel(wo[:, cg * 512:(cg + 1) * 512], KD, 512)
            for c4 in range(4):
                cc = 4 * cg + c4
                bank = nextbank()
                for k in range(KD):
                    yv, yk = self.yT(k)
                    S.op("pe", lambda e: e.matmul(self.ps[bank][:, 0:T], pv[:, k, c4 * 128:(c4 + 1) * 128], yv,
                                                  start=(k == 0), stop=(k == KD - 1)),
                         reads=[pk, yk], writes=[("ps", bank)], milestone=(k == KD - 1))
                while pending:
                    self.stat_mm(pending.pop(0), 4, 5)
                S.op("dve", lambda e: e.scalar_tensor_tensor(out=self.res[:, cc, :], in0=self.ps[bank][:, 0:T],
                                                             scalar=self.col("bout", cc), in1=self.res[:, cc, :],
                                                             op0=ALU.add, op1=ALU.add),
                     reads=[("ps", bank), ("res", cc), "cpp"], writes=[("res", cc)])
                self.stat_prep(cc)
                pending.append(cc)
        while pending:
            self.stat_mm(pending.pop(0), 4, 5)
        if FILL_LN:
            self.fillers(FILL_LN, 7)

    def res_load(self):
        c, S = self.cfg, self.S
        src = self.xT.ap()[:, self.cur_t0:self.cur_t0 + c.T].rearrange("(k p) n -> p k n", p=128)
        S.dma("sp", self.res[:], src, "xres", writes=[("res", k) for k in range(c.KD)])

    def build(self):
        c, S = self.cfg, self.S
        T, KD = c.T, c.KD
        self.deferred = []
        self.late_done = False
        self.stat_slot = {}
        self.setup()
        for ti in range(c.NT):
            t0 = ti * T
            self.cur_t0 = t0
            self.pid = 0
            self.conv_mod = (SCR_MOD, 0) if (ti == 0 and SCR_MOD) else None
            self.scr_new = []
            src = self.xT.ap()[:, t0:t0 + T].rearrange("(k p) n -> p k n", p=128)
            S.dma("pool", self.act[:], src, "xact", writes=[("act", k) for k in range(KD)])
            halo = (ti == 0 and self.use_halo)
            self.res_state = 0
            if not self.deferred:
                self.res_load()
                self.res_state = 1
                self.res_m = -4
            self.ffn("a", scale_res=True, halo=halo, fill=FILL_LN, first_tile=(ti == 0))
            self.layernorm(0, 1, "lnag", "lnab", False, t0)
            self.mixer(ti, halo=halo)
            self.layernorm(4, 5, "lnmg", "lnmb", False, t0)
            self.ffn("c", scale_res=False, kouter_first=True)
            self.layernorm(0, 1, "lncg", "lncb", True, t0)
            assert self.pid == self.NPAN, (self.pid, self.NPAN)
            self.scr_valid.update(self.scr_new)
        while self.deferred:
            self.deferred.pop(0)()
        S.wait_all("sp", [("res", cc) for cc in range(KD)])
        return self.nc


def host_inputs(cfg, inputs, core):
    c = cfg
    f = lambda a: np.ascontiguousarray(np.asarray(a, dtype=np.float32))
    x = np.asarray(inputs["x"], dtype=np.float32).reshape(c.BATCH * c.SEQ, c.D)
    r0 = core * c.NTOK
    m = {}
    m["xT"] = f(x[r0:r0 + c.NTOK].T)
    has_halo = (r0 % c.SEQ) != 0
    m["xh"] = f(x[r0 - 2:r0].T) if has_halo else np.zeros((c.D, 2), np.float32)
    m["wg_a"] = f(inputs["ffa_gate"][0]); m["wu_a"] = f(inputs["ffa_up"][0]); m["wd_a"] = f(inputs["ffa_down"][0])
    m["wg_c"] = f(inputs["ffc_gate"][0]); m["wu_c"] = f(inputs["ffc_up"][0]); m["wd_c"] = f(inputs["ffc_down"][0])
    m["w_in"] = f(inputs["w_in"][0]); m["w_out"] = f(inputs["w_out"][0])
    pp = lambda v: np.asarray(v, np.float32).reshape(-1, 128).T
    b_in = np.asarray(inputs["b_in"][0], np.float32)
    DA, DB = c.DA, c.DB
    cols = [pp(inputs["ln_a_g"][0]), pp(inputs["ln_a_b"][0]), pp(inputs["ln_m_g"][0]), pp(inputs["ln_m_b"][0]),
            pp(inputs["ln_c_g"][0]), pp(inputs["ln_c_b"][0]), pp(inputs["b_out"][0]),
            pp(b_in[0:DA]), pp(b_in[2 * DA:2 * DA + DB]), pp(b_in[2 * DA + DB:2 * DA + 2 * DB]), pp(b_in[2 * DA + 2 * DB:]),
            pp(np.asarray(inputs["conv_w"][0], np.float32)[0]), pp(np.asarray(inputs["conv_w"][0], np.float32)[1]),
            pp(np.asarray(inputs["conv_w"][0], np.float32)[2]),
            np.full((128, 1), 1.0 if has_halo else 0.0, np.float32)]
    m["cpp"] = f(np.concatenate(cols, axis=1))
    assert m["cpp"].shape == (128, c.NCC)
    m["rows"] = f(np.concatenate([b_in[DA:2 * DA], np.asarray(inputs["b_s"][0], np.float32).reshape(-1)])[None, :])
    m["gvbv"] = f(np.stack([np.asarray(inputs["ln_v_g"][0], np.float32), np.asarray(inputs["ln_v_b"][0], np.float32)]))
    ws = np.asarray(inputs["w_s"][0], np.float32)
    m["wsT"] = f(np.transpose(ws, (2, 0, 1)).reshape(128, c.NH * 128))
    return m


_CACHE = {}


def run(cfg, inputs, use_halo=True, trace=False):
    key = (cfg.D, cfg.F, cfg.SEQ, cfg.BATCH, cfg.NCORES, cfg.T, use_halo)
    if key not in _CACHE:
        _CACHE[key] = Prog(cfg, use_halo).build()
    nc = _CACHE[key]
    in_maps = [host_inputs(cfg, inputs, core) for core in range(cfg.NCORES)]
    res = run_bass_kernel_spmd(nc, in_maps, core_ids=list(range(cfg.NCORES)), **({"trace": True} if trace else {}))
    out = np.empty((cfg.BATCH * cfg.SEQ, cfg.D), np.float32)
    for core in range(cfg.NCORES):
        out[core * cfg.NTOK:(core + 1) * cfg.NTOK] = res.results[core]["outT"].T
    return out.reshape(cfg.BATCH, cfg.SEQ, cfg.D), res


def kernel(**inputs):
    cfg = Cfg()
    out, _ = run(cfg, inputs)
    return out
```
